# Optimizing a Trainium2 kernel written in Bass

```python
import math
import jax, jax.numpy as jnp
from jax import lax
import numpy as np

D_MODEL = 2048
BATCH = 8
SEQ = 2048
DEPTH = 1

CHUNK = 64
LEFT_CHUNKS = 8
BAND = LEFT_CHUNKS + 1
N_HEADS = 16
HEAD_DIM = 64
ATTN_WIDTH = N_HEADS * HEAD_DIM
MAX_REL = 128
POOL_WINDOWS = (2, 4, 8, 16)
N_POOL_GROUPS = len(POOL_WINDOWS)
POOL_WIDTH = D_MODEL // 2
POOL_GROUP = POOL_WIDTH // N_POOL_GROUPS
IN_WIDTH = 3 * ATTN_WIDTH + POOL_WIDTH + 2 * D_MODEL
D_FF = ((8 * D_MODEL + 3 * 256 - 1) // (3 * 256)) * 256
EPS = 1e-6
NEG_INF = -1e30

kernel_name = "hybrid_chunk_attn_pool_gated_block"


def rms_norm(x, g):
    xf = x.astype(jnp.float32)
    y = xf * lax.rsqrt(jnp.mean(xf * xf, axis=-1, keepdims=True) + EPS)
    return (y * g.astype(jnp.float32)).astype(x.dtype)


def chunked_rel_attention(q, k, v, rel_bias):
    B, S, _ = q.shape
    nc = S // CHUNK
    qc = q.reshape(B, nc, CHUNK, N_HEADS, HEAD_DIM)
    pad = ((0, 0), (LEFT_CHUNKS * CHUNK, 0), (0, 0))
    kp = jnp.pad(k, pad).reshape(B, nc + LEFT_CHUNKS, CHUNK, N_HEADS, HEAD_DIM)
    vp = jnp.pad(v, pad).reshape(B, nc + LEFT_CHUNKS, CHUNK, N_HEADS, HEAD_DIM)
    band_idx = jnp.arange(nc)[:, None] + jnp.arange(BAND)[None, :]
    kb = kp[:, band_idx].reshape(B, nc, BAND * CHUNK, N_HEADS, HEAD_DIM)
    vb = vp[:, band_idx].reshape(B, nc, BAND * CHUNK, N_HEADS, HEAD_DIM)
    scale = 1.0 / math.sqrt(HEAD_DIM)
    s = jnp.einsum('bnihd,bnjhd->bhnij', qc, kb).astype(jnp.float32) * scale
    i = jnp.arange(CHUNK)
    j = jnp.arange(BAND * CHUNK)
    dist = LEFT_CHUNKS * CHUNK + i[:, None] - j[None, :]
    rel_idx = jnp.clip(dist, -MAX_REL, MAX_REL) + MAX_REL
    bias = rel_bias.astype(jnp.float32)[:, rel_idx]
    key_pos = (jnp.arange(nc)[:, None] - LEFT_CHUNKS) * CHUNK + j[None, :]
    valid = key_pos >= 0
    s = s + bias[None, :, None]
    s = jnp.where(valid[None, None, :, None, :], s, NEG_INF)
    p = jax.nn.softmax(s, axis=-1).astype(v.dtype)
    o = jnp.einsum('bhnij,bnjhd->bnihd', p, vb)
    return o.reshape(B, S, ATTN_WIDTH)


def multiscale_pool(u, pool_w, pool_scale):
    B, S, _ = u.shape
    uf = u.astype(jnp.float32)
    cs = jnp.concatenate([jnp.zeros((B, 1, POOL_WIDTH), jnp.float32),
                          jnp.cumsum(uf, axis=1)], axis=1)
    t = jnp.arange(S)
    outs = []
    for g, w in enumerate(POOL_WINDOWS):
        sl = slice(g * POOL_GROUP, (g + 1) * POOL_GROUP)
        lo = jnp.maximum(t + 1 - w, 0)
        cnt = (t + 1 - lo).astype(jnp.float32)[None, :, None]
        mean = (cs[:, 1:, sl] - cs[:, lo, sl]) / cnt
        outs.append(mean - uf[..., sl])
    d = jnp.stack(outs, axis=2).astype(u.dtype)
    y = jnp.einsum('bsgc,gce->bsge', d, pool_w).reshape(B, S, POOL_WIDTH)
    return y * pool_scale


def swiglu(h, w_gate_up, w_down):
    gu = h @ w_gate_up
    gate, up = gu[..., :D_FF], gu[..., D_FF:]
    return (jax.nn.silu(gate) * up) @ w_down


def setup_inputs(seed: int = 0) -> dict:
    key = jax.random.key(seed)
    ks = jax.random.split(key, 16)
    f32 = jnp.float32

    def nrm(k, shape, fan_in):
        return jax.random.normal(k, shape, f32) * (fan_in ** -0.5)

    def gain(k, shape):
        return 1.0 + 0.02 * jax.random.normal(k, shape, f32)

    return {
        "x": jax.random.normal(ks[0], (BATCH, SEQ, D_MODEL), f32),
        "norm_mix": gain(ks[1], (DEPTH, D_MODEL)),
        "w_in": nrm(ks[2], (DEPTH, D_MODEL, IN_WIDTH), D_MODEL),
        "rel_bias": 0.1 * jax.random.normal(ks[3], (DEPTH, N_HEADS, 2 * MAX_REL + 1), f32),
        "pool_w": nrm(ks[4], (DEPTH, N_POOL_GROUPS, POOL_GROUP, POOL_GROUP), POOL_GROUP),
        "pool_scale": gain(ks[5], (DEPTH, POOL_WIDTH)),
        "w_branch_a": nrm(ks[6], (DEPTH, ATTN_WIDTH, D_MODEL), ATTN_WIDTH),
        "w_branch_b": nrm(ks[7], (DEPTH, POOL_WIDTH, D_MODEL), POOL_WIDTH),
        "w_out": nrm(ks[8], (DEPTH, D_MODEL, D_MODEL), D_MODEL),
        "norm_ffn": gain(ks[9], (DEPTH, D_MODEL)),
        "w_gate_up": nrm(ks[10], (DEPTH, D_MODEL, 2 * D_FF), D_MODEL),
        "w_down": nrm(ks[11], (DEPTH, D_FF, D_MODEL), D_FF),
        "norm_final": gain(ks[12], (D_MODEL,)),
    }


def reference(x, norm_mix, w_in, rel_bias, pool_w, pool_scale, w_branch_a, w_branch_b,
              w_out, norm_ffn, w_gate_up, w_down, norm_final):
    a0 = 3 * ATTN_WIDTH
    a1 = a0 + POOL_WIDTH
    a2 = a1 + D_MODEL
    for l in range(DEPTH):
        h = rms_norm(x, norm_mix[l])
        z = h @ w_in[l]
        q = z[..., :ATTN_WIDTH]
        k = z[..., ATTN_WIDTH:2 * ATTN_WIDTH]
        v = z[..., 2 * ATTN_WIDTH:a0]
        u = z[..., a0:a1]
        gate_a = jax.nn.sigmoid(z[..., a1:a2])
        gate_b = jax.nn.sigmoid(z[..., a2:])
        y_a = chunked_rel_attention(q, k, v, rel_bias[l])
        y_b = multiscale_pool(u, pool_w[l], pool_scale[l])
        merged = gate_a * (y_a @ w_branch_a[l]) + gate_b * (y_b @ w_branch_b[l])
        x = x + merged @ w_out[l]
        x = x + swiglu(rms_norm(x, norm_ffn[l]), w_gate_up[l], w_down[l])
    return rms_norm(x, norm_final)
```

```python
import numpy as np
import concourse.bass as bass
import concourse.mybir as mybir
from concourse.bass_utils import run_bass_kernel_spmd

F32 = mybir.dt.float32
BF16 = mybir.dt.bfloat16
AF = mybir.ActivationFunctionType
ALU = mybir.AluOpType

P = 128
D = 2048
KC = D // P
TB = 512
AW = 1024
AC = AW // P
NH = 16
DFF = 5632
FC = DFF // P
FH = FC // 2
INW = 8192
NW = 256
NSLOT = 5
EPS = 1e-6
A0 = 3 * AW
A1 = A0 + AW
A2 = A1 + D
BTW = 640
MASKV = -30000.0
SAME_ENGINE_SYNC = True


class Stream:
    def __init__(self, name, sem, step):
        self.name, self.sem, self.step = name, sem, step
        self.n = 0
        self.observed = set()
        self.rank = None
        self.snap = []

    def value(self, idx):
        if self.step == 16:
            return 16 * (idx + 1)
        return self.rank[idx]


class Tile:
    def __init__(self, name, space, lo, hi, ap):
        self.name, self.space, self.lo, self.hi = name, space, lo, hi
        self.ap = ap
        self.w = None
        self.r = {}
        self.over = []

    def __getitem__(self, key):
        return self.ap[key]


class Op:
    __slots__ = ("stream", "idx", "fn", "waits", "is_dma")

    def __init__(self, stream, idx, fn, waits, is_dma):
        self.stream, self.idx, self.fn, self.waits, self.is_dma = stream, idx, fn, waits, is_dma


class Sched:
    ENG = ("pe", "act", "dve", "pool", "sp")

    def __init__(self, nc):
        self.nc = nc
        self.streams = {}
        for e in ("pe", "act", "dve", "pool"):
            self.streams[e] = Stream(e, nc.alloc_semaphore("s_" + e), 1)
        self.prog = {e: [] for e in self.ENG}
        self.known = {e: {} for e in self.ENG}
        self.tiles = []

    def dma_stream(self, name):
        st = Stream(name, self.nc.alloc_semaphore("d_" + name), 16)
        self.streams[name] = st
        return st

    def add_tile(self, t):
        for o in self.tiles:
            if o.space == t.space and o.lo < t.hi and t.lo < o.hi:
                o.over.append(t)
                t.over.append(o)
        self.tiles.append(t)
        return t

    def op(self, issuer, fn, reads=(), writes=(), dma=None):
        st = self.streams[issuer] if dma is None else dma
        idx = st.n
        st.n += 1
        deps = {}
        raw = {}

        def need(si, d):
            s, i = si
            if d.get(s, -1) < i:
                d[s] = i

        for t in reads:
            if t.w is not None:
                need(t.w, raw)
            for o in t.over:
                if o.w is not None:
                    need(o.w, raw)
        for t in writes:
            for o in [t] + t.over:
                if o.w is not None:
                    need(o.w, deps)
                for s, i in o.r.items():
                    need((s, i), deps)
        kn = self.known[issuer]
        waits = []
        for d, is_raw in ((raw, True), (deps, False)):
            for s, i in d.items():
                if dma is None and s is st:
                    if issuer == "pe" or not SAME_ENGINE_SYNC:
                        continue
                if kn.get(s, -1) >= i:
                    continue
                kn[s] = i
                waits.append((s, i))
                s.observed.add(i)
                for s2, i2 in s.snap[i].items():
                    if kn.get(s2, -1) < i2:
                        kn[s2] = i2
        st.snap.append(dict(kn))
        for t in reads:
            if t.r.get(st, -1) < idx:
                t.r[st] = idx
        for t in writes:
            t.w = (st, idx)
            t.r = {}
        self.prog[issuer].append(Op(st, idx, fn, waits, dma is not None))
        return (st, idx)

    def final_wait(self, issuer, st_idx):
        s, i = st_idx
        s.observed.add(i)
        self.prog[issuer].append(Op(None, None, None, [(s, i)], False))

    def finalize(self):
        for s in self.streams.values():
            if s.step == 1:
                s.rank = {}
                for r, i in enumerate(sorted(s.observed)):
                    s.rank[i] = r + 1

    def replay(self, issuer, eng):
        for o in self.prog[issuer]:
            for s, i in o.waits:
                eng.wait_ge(s.sem, s.value(i))
            if o.fn is None:
                continue
            ins = o.fn(eng)
            if o.is_dma:
                ins.then_inc(o.stream.sem, 16)
            elif o.idx in o.stream.observed:
                ins.then_inc(o.stream.sem, 1)


def build_program(S, debug=False):
    NB = S // TB
    nc = bass.Bass("TRN2", target_bir_lowering=False)
    sc = Sched(nc)

    def dram_in(name, shape):
        return nc.dram_tensor(name, list(shape), F32, kind="ExternalInput").ap()

    xT_d = dram_in("xT", [D, S])
    w_in_d = dram_in("w_in", [D, INW])
    pool_w_d = dram_in("pool_w", [AW, 256])
    wa_d = dram_in("w_a", [AW, D])
    wb_d = dram_in("w_b", [AW, D])
    wo_d = dram_in("w_o", [D, D])
    wgu_d = dram_in("w_gu", [D, 2 * DFF])
    wdn_d = dram_in("w_dn", [DFF, D])
    bt_d = dram_in("bt", [P, NH, BTW])
    mask_d = dram_in("mask", [P, BTW])
    ident_d = dram_in("ident", [P, P])
    vecs_d = dram_in("vecs", [P, 3 * KC + AC + 64])
    yT_d = nc.dram_tensor("yT", [D, S], F32, kind="ExternalOutput").ap()
    eb_d = nc.dram_tensor("eb_scr", [NH, P, BTW], BF16, kind="Internal").ap()

    ARENA = 207 * 1024
    arena = nc.alloc_sbuf_tensor("arena", [P, ARENA // 2], BF16)
    cur = [0]

    def mk(name, nbytes, dtype, at=None):
        off = cur[0] if at is None else at
        assert off % 4 == 0 and nbytes % 4 == 0
        if at is None:
            cur[0] += nbytes
            assert cur[0] <= ARENA, (name, cur[0])
        ap = arena[:, off // 2:(off + nbytes) // 2]
        if dtype == F32:
            ap = ap.bitcast(F32)
        return sc.add_tile(Tile(name, "sb", off, off + nbytes, ap))

    xT = [mk(f"xT{k}", TB * 4, F32) for k in range(KC)]
    hT = [mk(f"hT{k}", TB * 2, BF16) for k in range(KC)]
    r1 = cur[0]
    qT = [mk(f"qT{c}", TB * 2, BF16) for c in range(AC)]
    dT = [mk(f"dT{c}", TB * 2, BF16) for c in range(AC)]
    yaT = [mk(f"yaT{c}", TB * 2, BF16) for c in range(AC)]
    ybT = [mk(f"ybT{c}", TB * 2, BF16) for c in range(AC)]
    r1_end = cur[0]
    mT = [mk(f"mT{k}", TB * 2, BF16, at=r1 + k * TB * 2) for k in range(KC)]
    aT = [mk(f"aT{i}", TB * 2, BF16, at=r1 + i * TB * 2) for i in range(FH)]
    assert r1 + FH * TB * 2 <= r1_end
    ri = r1 + 16 * 1024
    mask_t = mk("mask", BTW * 4, F32, at=ri)
    btst = [mk(f"btst{i}", BTW * 4, F32, at=ri + (i + 1) * BTW * 4) for i in range(2)]
    ebst = [mk(f"ebst{i}", BTW * 2, BF16, at=ri + 3 * BTW * 4 + i * BTW * 2) for i in range(2)]
    kT = [[mk(f"kT{s}_{c}", TB * 2, BF16) for c in range(AC)] for s in range(2)]
    Vt = [[[mk(f"V{s}_{t}_{h}", P * 2, BF16) for h in range(NH)] for t in range(4)] for s in range(2)]
    EBt = [mk(f"EB{i}", BTW * 2, BF16) for i in range(3)]
    wslot = [[mk(f"w{s}_{h}", 8 * NW * 2, BF16) for h in range(2)] for s in range(NSLOT)]
    wslot_ap = [arena[:, wslot[s][0].lo // 2: wslot[s][1].hi // 2] for s in range(NSLOT)]
    halo = [mk(f"halo{c}", 16 * 4, F32) for c in range(AC)]
    LAG = 4
    NPB = LAG + 2
    pB = [mk(f"pB{i}", TB * 2, BF16) for i in range(NPB)]
    ident = mk("ident", P * 2, BF16)
    rec = [mk(f"rec{i}", TB * 4, F32) for i in range(1)]
    sq = [mk(f"sq{i}", TB * 2, BF16) for i in range(2)]
    rstd = mk("rstd", TB * 4, F32)
    rt = rstd
    sgA = [mk(f"sgA{i}", TB * 4, F32) for i in range(2)]
    sgB = [mk(f"sgB{i}", TB * 4, F32) for i in range(2)]
    ub = mk("ub", (16 + TB) * 4, F32, at=sgA[0].lo)
    sA = mk("sA", (16 + TB) * 4, F32, at=sgA[0].lo + (16 + TB) * 4)
    sB = mk("sB", (16 + TB) * 4, F32, at=sgA[0].lo + 2 * (16 + TB) * 4)
    assert sgA[0].lo + 3 * (16 + TB) * 4 <= sgB[1].hi
    rstd_n = mk("rstd_n", TB * 4, F32, at=pB[0].lo)
    sqn = [mk(f"sqn{i}", TB * 2, BF16, at=pB[0].lo + TB * 4 + i * TB * 2) for i in range(4)]
    assert pB[0].lo + TB * 4 + 4 * TB * 2 <= pB[NPB - 1].hi
    stg = [mk(f"stg{i}", 2 * TB * 4, F32, at=r1 + 24 * 1024 + i * 2 * TB * 4) for i in range(2)]
    ostg = [mk(f"ostg{i}", TB * 4, F32) for i in range(2)]
    ones = mk("ones", P * 2, BF16)
    vecs = mk("vecs", (3 * KC + AC + 64) * 4, F32)
    tiny = mk("tiny", 16 * 4, F32)
    G_MIX, G_FFN, G_FIN, PSC, ICN = 0, KC, 2 * KC, 3 * KC, 3 * KC + AC

    banks = []
    for i in range(8):
        h = nc.alloc_psum_tensor(f"ps{i}", [P, TB], F32)
        banks.append(sc.add_tile(Tile(f"ps{i}", f"ps{i}", 0, 1, h[:, :])))
    bank_rr = [0]

    reserved = set()

    def next_bank():
        while True:
            b = banks[bank_rr[0] % 8]
            bank_rr[0] += 1
            if b.name not in reserved:
                return b

    st_c1, st_c2, st_c3 = sc.dma_stream("c1"), sc.dma_stream("c2"), sc.dma_stream("c3")
    st_x = [sc.dma_stream(f"x{i}") for i in range(4)]
    st_w = [sc.dma_stream(f"w{i}") for i in range(NSLOT)]
    st_o = [sc.dma_stream(f"o{i}") for i in range(2)]
    st_bt = [sc.dma_stream(f"bt{i}") for i in range(2)]
    st_pf = [sc.dma_stream(f"pf{i}") for i in range(2)]
    st_ebs = [sc.dma_stream(f"ebs{i}") for i in range(2)]
    st_ebl = [sc.dma_stream(f"ebl{i}") for i in range(3)]
    eb_dram = [sc.add_tile(Tile(f"ebd{h}", f"dram_eb{h}", 0, 1, None)) for h in range(NH)]

    def dma(issuer, stream, out_ap, in_ap, reads=(), writes=()):
        sc.op(issuer, lambda e, o=out_ap, i=in_ap: e.dma_start(out=o, in_=i), reads, writes, dma=stream)

    def mm(bank, cols, lhsT, rhs, start, stop, reads):
        c0, c1 = cols
        sc.op("pe", lambda e, o=bank.ap[:, c0:c1], l=lhsT, r=rhs, s=start, t=stop:
              e.matmul(o, l, r, start=s, stop=t), reads, [bank])

    def mm_rows(bank, rows, cols, lhsT, rhs, start, stop, reads):
        c0, c1 = cols
        sc.op("pe", lambda e, o=bank.ap[rows[0]:rows[1], c0:c1], l=lhsT, r=rhs, s=start, t=stop:
              e.matmul(o, l, r, start=s, stop=t), reads, [bank])

    def act(out_ap, in_ap, func, reads, writes, scale=1.0, bias=0.0):
        sc.op("act", lambda e, o=out_ap, i=in_ap, f=func, s=scale, b=bias:
              e.activation(out=o, in_=i, func=f, bias=b, scale=s), reads, writes)

    def dve_tt(out_ap, in0, in1, op, reads, writes):
        sc.op("dve", lambda e, o=out_ap, a=in0, b=in1, p=op: e.tensor_tensor(out=o, in0=a, in1=b, op=p),
              reads, writes)

    def dve_ts(out_ap, in0, s1, op0, reads, writes):
        sc.op("dve", lambda e, o=out_ap, a=in0, s=s1, p=op0:
              e.tensor_scalar(out=o, in0=a, scalar1=s, scalar2=None, op0=p), reads, writes)

    def dve_stt(out_ap, in0, scalar, in1, op0, op1, reads, writes):
        sc.op("dve", lambda e, o=out_ap, a=in0, s=scalar, b=in1, p0=op0, p1=op1:
              e.scalar_tensor_tensor(out=o, in0=a, scalar=s, in1=b, op0=p0, op1=p1), reads, writes)

    def dve_copy(out_ap, in_ap, reads, writes):
        sc.op("dve", lambda e, o=out_ap, i=in_ap: e.tensor_copy(out=o, in_=i), reads, writes)

    wcount = [0]

    class WT:
        def __init__(self, s, nk):
            self.s, self.nk = s, nk

        def lhsT(self, kc, c0, c1):
            return wslot_ap[self.s][:, kc * NW + c0: kc * NW + c1]

        def tile(self, kc):
            return wslot[self.s][0 if kc < 8 else 1]

    def wload(w_d, ncols, k0, nk, n0, nw=NW):
        s = wcount[0] % NSLOT
        wcount[0] += 1
        for h0 in range(0, nk, 8):
            n = min(8, nk - h0)
            src = bass.AP(w_d.tensor, (k0 + h0) * P * ncols + n0, [[ncols, P], [P * ncols, n], [1, nw]])
            dst = wslot_ap[s][:, h0 * NW:(h0 + n) * NW].rearrange("p (k n) -> p k n", n=NW)
            if nw != NW:
                dst = dst[:, :, 0:nw]
            dma("pool", st_w[s], dst, src, writes=[wslot[s][h0 // 8]])
        last = (st_w[s], st_w[s].n - 1)
        for h0 in range(0, nk, 8):
            wslot[s][h0 // 8].w = last
        return WT(s, nk)

    dma("pool", st_c1, ident.ap, ident_d[:, :], writes=[ident])
    dma("sp", st_c2, vecs.ap, vecs_d[:, :], writes=[vecs])
    dma("sp", st_c3, mask_t.ap, mask_d[:, :], writes=[mask_t])
    sc.op("dve", lambda e: e.memset(ones.ap, 1.0), [], [ones])
    for c in range(AC):
        sc.op("dve", lambda e, c=c: e.memset(halo[c].ap, 0.0), [], [halo[c]])
    def eb_init(h):
        i = h % 2
        dma("sp", st_bt[i], btst[i].ap, bt_d[:, h, :], writes=[btst[i]])
        dve_tt(ebst[i].ap, btst[i].ap, mask_t.ap, ALU.add, [btst[i], mask_t], [ebst[i]])
        dma("sp", st_ebs[i], eb_d[h, :, :], ebst[i].ap, reads=[ebst[i]], writes=[eb_dram[h]])

    def v_init():
        for s_ in range(2):
            for t_ in range(4):
                vv = arena[:, Vt[s_][t_][0].lo // 2: Vt[s_][t_][NH - 1].hi // 2]
                sc.op("dve", lambda e, v=vv: e.memset(v, 1.0), [], Vt[s_][t_])

    def gcol(base, k):
        return vecs.ap[:, base + k: base + k + 1]

    def rmsnorm(gbase, dst_fn):
        bank = next_bank()
        for k in range(KC):
            s = sq[k % 2]
            act(s.ap, xT[k].ap, AF.Square, [xT[k]], [s])
            mm(bank, (0, TB), ones.ap, s.ap, k == 0, k == KC - 1, [ones, s])
        act(rt.ap, bank.ap, AF.Sqrt, [bank], [rt], scale=1.0 / D, bias=EPS)
        sc.op("dve", lambda e: e.reciprocal(out=rstd.ap, in_=rt.ap), [rt], [rstd])
        for k in range(KC):
            dst_fn(k)

    ocount = [0]
    for b in range(NB):
        sl = b % 2
        c0 = b * TB
        for i in range(4):
            src = bass.AP(xT_d.tensor, (4 * i * P) * S + c0, [[S, P], [P * S, 4], [1, TB]])
            dst = arena[:, xT[4 * i].lo // 2: xT[4 * i + 3].hi // 2].bitcast(F32).rearrange(
                "p (k n) -> p k n", n=TB)
            dma("sp", st_x[i], dst, src, writes=xT[4 * i:4 * i + 4])

        def norm1_dst(k):
            dve_stt(hT[k].ap, xT[k].ap, gcol(G_MIX, k), rstd.ap, ALU.mult, ALU.mult,
                    [xT[k], vecs, rstd], [hT[k]])
        if b == 0:
            rmsnorm(G_MIX, norm1_dst)
            v_init()

        for t in range(16):
            w = wload(w_in_d, INW, 0, KC, t * NW)
            kind = t // 4
            if b == 0:
                eb_init(t)
            if kind == 2:
                for tt in range(4):
                    bank = next_bank()
                    for k in range(KC):
                        mm(bank, (0, NW), hT[k].ap[:, tt * P:(tt + 1) * P], w.lhsT(k, 0, NW),
                           k == 0, k == KC - 1, [hT[k], w.tile(k)])
                    for hh in range(4):
                        hd = (t % 4) * 4 + hh
                        cc = (hd % 2) * 64
                        act(Vt[sl][tt][hd].ap[:, cc:cc + 64], bank.ap[:, hh * 64:(hh + 1) * 64], AF.Identity,
                            [bank], [Vt[sl][tt][hd]])
                continue
            for oc in range(2):
                c = (t % 4) * 2 + oc
                bank = next_bank()
                for k in range(KC):
                    mm(bank, (0, TB), w.lhsT(k, oc * P, (oc + 1) * P), hT[k].ap,
                       k == 0, k == KC - 1, [hT[k], w.tile(k)])
                if kind == 0:
                    act(qT[c].ap, bank.ap, AF.Identity, [bank], [qT[c]], scale=0.125)
                elif kind == 1:
                    dve_copy(kT[sl][c].ap, bank.ap, [bank], [kT[sl][c]])
                else:
                    g = c // 2
                    wdw = 2 ** (g + 1)
                    dve_copy(ub.ap[:, 0:16], halo[c].ap, [halo[c]], [ub])
                    dve_copy(ub.ap[:, 16:16 + TB], bank.ap, [bank], [ub])
                    dve_copy(halo[c].ap, ub.ap[:, TB:TB + 16], [ub], [halo[c]])
                    L = 16 + TB
                    dve_tt(sA.ap[:, 1:L], ub.ap[:, 1:L], ub.ap[:, 0:L - 1], ALU.add, [ub], [sA])
                    last = sA
                    if wdw >= 4:
                        dve_tt(sB.ap[:, 3:L], sA.ap[:, 3:L], sA.ap[:, 1:L - 2], ALU.add, [sA], [sB])
                        last = sB
                    if wdw >= 8:
                        dve_tt(sA.ap[:, 7:L], sB.ap[:, 7:L], sB.ap[:, 3:L - 4], ALU.add, [sB], [sA])
                        last = sA
                    if wdw >= 16:
                        dve_tt(sB.ap[:, 15:L], sA.ap[:, 15:L], sA.ap[:, 7:L - 8], ALU.add, [sA], [sB])
                        last = sB
                    dve_stt(dT[c].ap, last.ap[:, 16:L], 1.0 / wdw, ub.ap[:, 16:L], ALU.mult, ALU.subtract,
                            [last, ub], [dT[c]])
                    if b == 0:
                        dve_tt(tiny.ap, last.ap[:, 16:32], vecs.ap[:, ICN + 16 * g: ICN + 16 * g + 16],
                               ALU.mult, [last, vecs], [tiny])
                        dve_tt(dT[c].ap[:, 0:16], tiny.ap, ub.ap[:, 16:32], ALU.subtract, [tiny, ub], [dT[c]])

        steps = []
        for h in range(NH):
            order = [4 * b, 4 * b - 1, 4 * b - 2, 4 * b - 3, 4 * b - 4, 4 * b + 1, 4 * b + 2, 4 * b + 3]
            order = [kt for kt in order if kt >= 0]
            for n_i, kt in enumerate(order):
                steps.append((h, kt, n_i == 0, n_i == len(order) - 1))
        Sb = banks[0:4]

        def geom(h, kt):
            j, half = h // 2, h % 2
            r0, r1_ = 64 * half, 64 * half + 64
            off = TB * b - P * kt
            f_lo, f_hi = max(off, 0), min(off + TB, BTW)
            return j, r0, r1_, f_lo, f_hi, f_lo - off, f_hi - off, (kt // 4) % 2

        def eb_load(h):
            dma("sp", st_ebl[h % 3], EBt[h % 3].ap, eb_d[h, :, :], reads=[eb_dram[h]], writes=[EBt[h % 3]])

        eb_load(0)
        eb_load(1)
        GS = 2
        LG = LAG // GS
        ngr = len(steps) // GS
        assert len(steps) % GS == 0
        for gi in range(ngr + LG):
            if gi < ngr:
                for it in range(gi * GS, (gi + 1) * GS):
                    h, kt, first, lastk = steps[it]
                    if first and h + 2 < NH:
                        eb_load(h + 2)
                    j, r0, r1_, f_lo, f_hi, q_lo, q_hi, ksl = geom(h, kt)
                    kc_ = (kt % 4) * P
                    sb_, pb_ = Sb[it % 4], pB[it % NPB]
                    eb = EBt[h % 3]
                    mm(sb_, (q_lo, q_hi), kT[ksl][j].ap[r0:r1_, kc_:kc_ + P], qT[j].ap[r0:r1_, q_lo:q_hi],
                       True, False, [kT[ksl][j], qT[j]])
                    mm(sb_, (q_lo, q_hi), ident.ap, eb.ap[:, f_lo:f_hi], False, True, [ident, eb])
                for it in range(gi * GS, (gi + 1) * GS):
                    h, kt, first, lastk = steps[it]
                    j, r0, r1_, f_lo, f_hi, q_lo, q_hi, ksl = geom(h, kt)
                    sb_, pb_ = Sb[it % 4], pB[it % NPB]
                    act(pb_.ap[:, q_lo:q_hi], sb_.ap[:, q_lo:q_hi], AF.Exp, [sb_], [pb_])
            g2 = gi - LG
            if g2 >= 0:
                for i2 in range(g2 * GS, (g2 + 1) * GS):
                    h, kt, first, lastk = steps[i2]
                    j, r0, r1_, f_lo, f_hi, q_lo, q_hi, ksl = geom(h, kt)
                    pb_ = pB[i2 % NPB]
                    Ob = banks[4 + (h % 4)]
                    vtile = Vt[ksl][kt % 4][h]
                    d0, d1 = (64, 128) if h % 2 == 0 else (0, 64)
                    mm(Ob, (q_lo, q_hi), vtile.ap, pb_.ap[:, q_lo:q_hi], first, lastk,
                       [vtile, pb_])
                    if lastk:
                        rc = rec[0]
                        sc.op("dve", lambda e, o=rc.ap[r0:r1_, :], i=Ob.ap[d0:d1, :]: e.reciprocal(out=o, in_=i),
                              [Ob], [rc])
                        dve_tt(yaT[j].ap[r0:r1_, :], Ob.ap[r0:r1_, :], rc.ap[r0:r1_, :], ALU.mult,
                               [Ob, rc], [yaT[j]])

        for g in range(4):
            w = wload(pool_w_d, 256, 2 * g, 2, 0)
            for ec in range(2):
                c = 2 * g + ec
                bank = next_bank()
                for k in range(2):
                    mm(bank, (0, TB), w.lhsT(k, ec * P, (ec + 1) * P), dT[2 * g + k].ap,
                       k == 0, k == 1, [dT[2 * g + k], w.tile(k)])
                dve_ts(ybT[c].ap, bank.ap, gcol(PSC, c), ALU.mult, [bank, vecs], [ybT[c]])

        for t in range(8):
            wga = wload(w_in_d, INW, 0, KC, A1 + t * NW)
            wgb = wload(w_in_d, INW, 0, KC, A2 + t * NW)
            wpa = wload(wa_d, D, 0, AC, t * NW)
            wpb = wload(wb_d, D, 0, AC, t * NW)
            for (wt, src, nk, kind) in ((wga, hT, KC, 0), (wgb, hT, KC, 1), (wpa, yaT, AC, 2), (wpb, ybT, AC, 3)):
                for oc in range(2):
                    n = 2 * t + oc
                    cs = (oc * P, (oc + 1) * P)
                    bk = next_bank()
                    for k in range(nk):
                        mm(bk, (0, TB), wt.lhsT(k, *cs), src[k].ap, k == 0, k == nk - 1, [src[k], wt.tile(k)])
                    if kind == 0:
                        act(sgA[oc].ap, bk.ap, AF.Sigmoid, [bk], [sgA[oc]])
                    elif kind == 1:
                        act(sgB[oc].ap, bk.ap, AF.Sigmoid, [bk], [sgB[oc]])
                    elif kind == 2:
                        dve_tt(sgA[oc].ap, bk.ap, sgA[oc].ap, ALU.mult, [bk, sgA[oc]], [sgA[oc]])
                    else:
                        dve_tt(sgB[oc].ap, bk.ap, sgB[oc].ap, ALU.mult, [bk, sgB[oc]], [sgB[oc]])
                        dve_tt(mT[n].ap, sgA[oc].ap, sgB[oc].ap, ALU.add, [sgA[oc], sgB[oc]], [mT[n]])

        for t in range(8):
            w = wload(wo_d, D, 0, KC, t * NW)
            for oc in range(2):
                n = 2 * t + oc
                bank = next_bank()
                for k in range(KC):
                    mm(bank, (0, TB), w.lhsT(k, oc * P, (oc + 1) * P), mT[k].ap,
                       k == 0, k == KC - 1, [mT[k], w.tile(k)])
                dve_tt(xT[n].ap, bank.ap, xT[n].ap, ALU.add, [bank, xT[n]], [xT[n]])

        def norm2_dst(k):
            dve_stt(hT[k].ap, xT[k].ap, gcol(G_FFN, k), rstd.ap, ALU.mult, ALU.mult,
                    [xT[k], vecs, rstd], [hT[k]])
        rmsnorm(G_FFN, norm2_dst)

        for hf in range(2):
            for tt in range(FH // 2):
                tg = hf * (FH // 2) + tt
                wg = wload(wgu_d, 2 * DFF, 0, KC, tg * NW)
                wu = wload(wgu_d, 2 * DFF, 0, KC, DFF + tg * NW)
                for oc in range(2):
                    i = 2 * tt + oc
                    cs = (oc * P, (oc + 1) * P)
                    bg, bu = next_bank(), next_bank()
                    for k in range(KC):
                        mm(bg, (0, TB), wg.lhsT(k, *cs), hT[k].ap, k == 0, k == KC - 1, [hT[k], wg.tile(k)])
                    for k in range(KC):
                        mm(bu, (0, TB), wu.lhsT(k, *cs), hT[k].ap, k == 0, k == KC - 1, [hT[k], wu.tile(k)])
                    sg = sgA[i % 2]
                    act(sg.ap, bg.ap, AF.Silu, [bg], [sg])
                    dve_tt(aT[i].ap, bu.ap, sg.ap, ALU.mult, [bu, sg], [aT[i]])
            pre = (hf == 1 and b + 1 < NB)
            if pre:
                pf_bank = next_bank()
                reserved.add(pf_bank.name)

            def pf_dma(item):
                e = item % 8
                src = bass.AP(xT_d.tensor, (2 * e * P) * S + (b + 1) * TB, [[S, P], [P * S, 2], [1, TB]])
                dst = stg[e % 2].ap.rearrange("p (k n) -> p k n", n=TB)
                dma("sp", st_pf[e % 2], dst, src, writes=[stg[e % 2]])

            def pf_norm(e):
                for kk in range(2):
                    k = 2 * e + kk
                    dve_stt(hT[k].ap, stg[e % 2].ap[:, kk * TB:(kk + 1) * TB], gcol(G_MIX, k), rstd_n.ap,
                            ALU.mult, ALU.mult, [stg[e % 2], vecs, rstd_n], [hT[k]])

            for t in range(8):
                wd0 = wload(wdn_d, D, hf * FH, 11, t * NW)
                wd1 = wload(wdn_d, D, hf * FH + 11, 11, t * NW)
                for oc in range(2):
                    n = 2 * t + oc
                    g = n
                    if pre:
                        if 1 <= g <= 8:
                            for kk in range(2):
                                sqt = sqn[(2 * (g - 1) + kk) % 4]
                                act(sqt.ap, stg[(g - 1) % 2].ap[:, kk * TB:(kk + 1) * TB], AF.Square,
                                    [stg[(g - 1) % 2]], [sqt])
                        if g >= 10:
                            pf_norm(g - 10)
                        pf_dma(g)
                    cs = (oc * P, (oc + 1) * P)
                    bank = next_bank()
                    for i in range(FH):
                        w = wd0 if i < 11 else wd1
                        k = i if i < 11 else i - 11
                        mm(bank, (0, TB), w.lhsT(k, *cs), aT[i].ap, i == 0, i == FH - 1, [aT[i], w.tile(k)])
                    dve_tt(xT[n].ap, bank.ap, xT[n].ap, ALU.add, [bank, xT[n]], [xT[n]])
                    if pre and 2 <= g <= 9:
                        e = g - 2
                        for kk in range(2):
                            sqt = sqn[(2 * e + kk) % 4]
                            mm(pf_bank, (0, TB), ones.ap, sqt.ap, e == 0 and kk == 0, e == 7 and kk == 1,
                               [ones, sqt])
                        if g == 9:
                            act(rt.ap, pf_bank.ap, AF.Sqrt, [pf_bank], [rt], scale=1.0 / D, bias=EPS)
                            sc.op("dve", lambda e_: e_.reciprocal(out=rstd_n.ap, in_=rt.ap), [rt], [rstd_n])
                            reserved.discard(pf_bank.name)
            if pre:
                pf_norm(6)
                pf_norm(7)

        def fin_dst(k):
            o = ostg[ocount[0] % 2]
            st = st_o[ocount[0] % 2]
            ocount[0] += 1
            dve_stt(o.ap, xT[k].ap, gcol(G_FIN, k), rstd.ap, ALU.mult, ALU.mult, [xT[k], vecs, rstd], [o])
            dma("sp", st, yT_d[k * P:(k + 1) * P, c0:c0 + TB], o.ap, reads=[o])
        rmsnorm(G_FIN, fin_dst)

    for st in st_o:
        if st.n > 0:
            sc.final_wait("sp", (st, st.n - 1))

    sc.finalize()
    with nc.allow_low_precision("bf16 matmul operands, fp32 accumulation"):
        with nc.Block() as block:
            @block.tensor
            def _(e):
                sc.replay("pe", e)

            @block.scalar
            def _(e):
                sc.replay("act", e)

            @block.vector
            def _(e):
                sc.replay("dve", e)

            @block.gpsimd
            def _(e):
                sc.replay("pool", e)

            @block.sync
            def _(e):
                sc.replay("sp", e)
    return nc


def _consts():
    p = np.arange(P)[:, None]
    f = np.arange(BTW)[None, :]
    dist = f - p
    idx = np.clip(dist, -128, 128) + 128
    vis = ((p // 64) <= (f // 64)) & ((f // 64) <= (p // 64) + 8)
    mask = np.where(vis, 0.0, MASKV).astype(np.float32)
    ident = np.eye(P, dtype=np.float32)
    invc = np.zeros((4, 16), np.float32)
    for g, w in enumerate((2, 4, 8, 16)):
        invc[g] = 1.0 / np.minimum(np.arange(16) + 1, w)
    return idx, mask, ident, invc


def _prep_shared(norm_mix, w_in, rel_bias, pool_w, pool_scale, w_branch_a, w_branch_b, w_out,
                 norm_ffn, w_gate_up, w_down, norm_final):
    idx, mask, ident, invc = _consts()
    f32 = np.float32
    bt = np.ascontiguousarray(np.transpose(np.asarray(rel_bias[0], f32)[:, idx], (1, 0, 2)))

    def cols(v, n):
        return np.asarray(v, f32).reshape(n, P).T

    vecs = np.concatenate([cols(norm_mix[0], KC), cols(norm_ffn[0], KC), cols(norm_final, KC),
                           cols(pool_scale[0], AC), np.broadcast_to(invc.reshape(1, 64), (P, 64))], axis=1)
    return {
        "w_in": np.ascontiguousarray(np.asarray(w_in[0], f32)),
        "pool_w": np.ascontiguousarray(np.asarray(pool_w[0], f32).reshape(AW, 256)),
        "w_a": np.ascontiguousarray(np.asarray(w_branch_a[0], f32)),
        "w_b": np.ascontiguousarray(np.asarray(w_branch_b[0], f32)),
        "w_o": np.ascontiguousarray(np.asarray(w_out[0], f32)),
        "w_gu": np.ascontiguousarray(np.asarray(w_gate_up[0], f32)),
        "w_dn": np.ascontiguousarray(np.asarray(w_down[0], f32)),
        "bt": bt, "mask": mask, "ident": ident,
        "vecs": np.ascontiguousarray(vecs.astype(f32)),
    }


_NC_CACHE = {}


def kernel(x, norm_mix, w_in, rel_bias, pool_w, pool_scale, w_branch_a, w_branch_b,
           w_out, norm_ffn, w_gate_up, w_down, norm_final):
    x = np.asarray(x, np.float32)
    B, S, _ = x.shape
    shared = _prep_shared(norm_mix, w_in, rel_bias, pool_w, pool_scale, w_branch_a, w_branch_b,
                          w_out, norm_ffn, w_gate_up, w_down, norm_final)
    if S not in _NC_CACHE:
        _NC_CACHE[S] = build_program(S)
    nc = _NC_CACHE[S]
    in_maps = []
    for c in range(B):
        m = dict(shared)
        m["xT"] = np.ascontiguousarray(x[c].T)
        in_maps.append(m)
    res = run_bass_kernel_spmd(nc, in_maps, core_ids=list(range(B)))
    out = np.stack([np.ascontiguousarray(r["yT"].T) for r in res.results], axis=0)
    return out.astype(np.float32)
```

```python
import numpy as np
import concourse.bass as bass
import concourse.mybir as mybir
from concourse.bass_utils import run_bass_kernel_spmd

F32 = mybir.dt.float32
BF16 = mybir.dt.bfloat16
AF = mybir.ActivationFunctionType
ALU = mybir.AluOpType

P = 128
D = 2048
KC = D // P
TB = 512
AW = 1024
AC = AW // P
NH = 16
DFF = 5632
FC = DFF // P
FH = FC // 2
INW = 8192
NW = 256
NSLOT = 5
EPS = 1e-6
A0 = 3 * AW
A1 = A0 + AW
A2 = A1 + D
BTW = 640
MASKV = -30000.0
SAME_ENGINE_SYNC = True


class Stream:
    def __init__(self, name, sem, step):
        self.name, self.sem, self.step = name, sem, step
        self.n = 0
        self.observed = set()
        self.rank = None
        self.snap = []

    def value(self, idx):
        if self.step == 16:
            return 16 * (idx + 1)
        return self.rank[idx]


class Tile:
    def __init__(self, name, space, lo, hi, ap):
        self.name, self.space, self.lo, self.hi = name, space, lo, hi
        self.ap = ap
        self.w = None
        self.r = {}
        self.over = []

    def __getitem__(self, key):
        return self.ap[key]


class Op:
    __slots__ = ("stream", "idx", "fn", "waits", "is_dma")

    def __init__(self, stream, idx, fn, waits, is_dma):
        self.stream, self.idx, self.fn, self.waits, self.is_dma = stream, idx, fn, waits, is_dma


class Sched:
    ENG = ("pe", "act", "dve", "pool", "sp")

    def __init__(self, nc):
        self.nc = nc
        self.streams = {}
        for e in ("pe", "act", "dve", "pool"):
            self.streams[e] = Stream(e, nc.alloc_semaphore("s_" + e), 1)
        self.prog = {e: [] for e in self.ENG}
        self.known = {e: {} for e in self.ENG}
        self.tiles = []

    def dma_stream(self, name):
        st = Stream(name, self.nc.alloc_semaphore("d_" + name), 16)
        self.streams[name] = st
        return st

    def add_tile(self, t):
        for o in self.tiles:
            if o.space == t.space and o.lo < t.hi and t.lo < o.hi:
                o.over.append(t)
                t.over.append(o)
        self.tiles.append(t)
        return t

    def op(self, issuer, fn, reads=(), writes=(), dma=None):
        st = self.streams[issuer] if dma is None else dma
        idx = st.n
        st.n += 1
        deps = {}
        raw = {}

        def need(si, d):
            s, i = si
            if d.get(s, -1) < i:
                d[s] = i

        for t in reads:
            if t.w is not None:
                need(t.w, raw)
            for o in t.over:
                if o.w is not None:
                    need(o.w, raw)
        for t in writes:
            for o in [t] + t.over:
                if o.w is not None:
                    need(o.w, deps)
                for s, i in o.r.items():
                    need((s, i), deps)
        kn = self.known[issuer]
        waits = []
        for d, is_raw in ((raw, True), (deps, False)):
            for s, i in d.items():
                if dma is None and s is st:
                    if issuer == "pe" or not SAME_ENGINE_SYNC or not is_raw:
                        continue
                if kn.get(s, -1) >= i:
                    continue
                kn[s] = i
                waits.append((s, i))
                s.observed.add(i)
                for s2, i2 in s.snap[i].items():
                    if kn.get(s2, -1) < i2:
                        kn[s2] = i2
        st.snap.append(dict(kn))
        for t in reads:
            if t.r.get(st, -1) < idx:
                t.r[st] = idx
        for t in writes:
            t.w = (st, idx)
            t.r = {}
        self.prog[issuer].append(Op(st, idx, fn, waits, dma is not None))
        return (st, idx)

    def final_wait(self, issuer, st_idx):
        s, i = st_idx
        s.observed.add(i)
        self.prog[issuer].append(Op(None, None, None, [(s, i)], False))

    def finalize(self):
        for s in self.streams.values():
            if s.step == 1:
                s.rank = {}
                for r, i in enumerate(sorted(s.observed)):
                    s.rank[i] = r + 1

    def replay(self, issuer, eng):
        for o in self.prog[issuer]:
            for s, i in o.waits:
                eng.wait_ge(s.sem, s.value(i))
            if o.fn is None:
                continue
            ins = o.fn(eng)
            if o.is_dma:
                ins.then_inc(o.stream.sem, 16)
            elif o.idx in o.stream.observed:
                ins.then_inc(o.stream.sem, 1)


def build_program(S, debug=False):
    NB = S // TB
    nc = bass.Bass("TRN2", target_bir_lowering=False)
    sc = Sched(nc)

    def dram_in(name, shape):
        return nc.dram_tensor(name, list(shape), F32, kind="ExternalInput").ap()

    xT_d = dram_in("xT", [D, S])
    w_in_d = dram_in("w_in", [D, INW])
    pool_w_d = dram_in("pool_w", [AW, 256])
    wa_d = dram_in("w_a", [AW, D])
    wb_d = dram_in("w_b", [AW, D])
    wo_d = dram_in("w_o", [D, D])
    wgu_d = dram_in("w_gu", [D, 2 * DFF])
    wdn_d = dram_in("w_dn", [DFF, D])
    bt_d = dram_in("bt", [P, NH, BTW])
    mask_d = dram_in("mask", [P, BTW])
    ident_d = dram_in("ident", [P, P])
    vecs_d = dram_in("vecs", [P, 3 * KC + AC + 64])
    yT_d = nc.dram_tensor("yT", [D, S], F32, kind="ExternalOutput").ap()
    eb_d = nc.dram_tensor("eb_scr", [NH, P, BTW], BF16, kind="Internal").ap()

    ARENA = 207 * 1024
    arena = nc.alloc_sbuf_tensor("arena", [P, ARENA // 2], BF16)
    cur = [0]

    def mk(name, nbytes, dtype, at=None):
        off = cur[0] if at is None else at
        assert off % 4 == 0 and nbytes % 4 == 0
        if at is None:
            cur[0] += nbytes
            assert cur[0] <= ARENA, (name, cur[0])
        ap = arena[:, off // 2:(off + nbytes) // 2]
        if dtype == F32:
            ap = ap.bitcast(F32)
        return sc.add_tile(Tile(name, "sb", off, off + nbytes, ap))

    xT = [mk(f"xT{k}", TB * 4, F32) for k in range(KC)]
    hT = [mk(f"hT{k}", TB * 2, BF16) for k in range(KC)]
    r1 = cur[0]
    qT = [mk(f"qT{c}", TB * 2, BF16) for c in range(AC)]
    dT = [mk(f"dT{c}", TB * 2, BF16) for c in range(AC)]
    yaT = [mk(f"yaT{c}", TB * 2, BF16) for c in range(AC)]
    ybT = [mk(f"ybT{c}", TB * 2, BF16) for c in range(AC)]
    r1_end = cur[0]
    mT = [mk(f"mT{k}", TB * 2, BF16, at=r1 + k * TB * 2) for k in range(KC)]
    aT = [mk(f"aT{i}", TB * 2, BF16, at=r1 + i * TB * 2) for i in range(FH)]
    assert r1 + FH * TB * 2 <= r1_end
    ri = r1 + 16 * 1024
    mask_t = mk("mask", BTW * 4, F32, at=ri)
    btst = [mk(f"btst{i}", BTW * 4, F32, at=ri + (i + 1) * BTW * 4) for i in range(2)]
    ebst = [mk(f"ebst{i}", BTW * 2, BF16, at=ri + 3 * BTW * 4 + i * BTW * 2) for i in range(2)]
    kT = [[mk(f"kT{s}_{c}", TB * 2, BF16) for c in range(AC)] for s in range(2)]
    Vt = [[[mk(f"V{s}_{t}_{h}", P * 2, BF16) for h in range(NH)] for t in range(4)] for s in range(2)]
    EBt = [mk(f"EB{i}", BTW * 2, BF16) for i in range(3)]
    wslot = [[mk(f"w{s}_{h}", 8 * NW * 2, BF16) for h in range(2)] for s in range(NSLOT)]
    wslot_ap = [arena[:, wslot[s][0].lo // 2: wslot[s][1].hi // 2] for s in range(NSLOT)]
    halo = [mk(f"halo{c}", 16 * 4, F32) for c in range(AC)]
    LAG = 6
    NPB = LAG + 2
    pB = [mk(f"pB{i}", TB * 2, BF16) for i in range(NPB)]
    ident = mk("ident", P * 2, BF16)
    rec = [mk(f"rec{i}", TB * 4, F32) for i in range(1)]
    sq = [mk(f"sq{i}", TB * 2, BF16) for i in range(2)]
    rstd = mk("rstd", TB * 4, F32)
    rt = rstd
    sgA = [mk(f"sgA{i}", TB * 4, F32) for i in range(2)]
    sgB = [mk(f"sgB{i}", TB * 4, F32) for i in range(2)]
    ub = mk("ub", (16 + TB) * 4, F32, at=sgA[0].lo)
    sA = mk("sA", (16 + TB) * 4, F32, at=sgA[0].lo + (16 + TB) * 4)
    sB = mk("sB", (16 + TB) * 4, F32, at=sgA[0].lo + 2 * (16 + TB) * 4)
    assert sgA[0].lo + 3 * (16 + TB) * 4 <= sgB[1].hi
    rstd_n = mk("rstd_n", TB * 4, F32, at=pB[0].lo)
    sqn = [mk(f"sqn{i}", TB * 2, BF16, at=pB[0].lo + TB * 4 + i * TB * 2) for i in range(4)]
    assert pB[0].lo + TB * 4 + 4 * TB * 2 <= pB[NPB - 1].hi
    stg = [mk(f"stg{i}", 2 * TB * 4, F32, at=r1 + 24 * 1024 + i * 2 * TB * 4) for i in range(2)]
    ones = mk("ones", P * 2, BF16)
    vecs = mk("vecs", (3 * KC + AC + 64) * 4, F32)
    tiny = mk("tiny", 16 * 4, F32)
    G_MIX, G_FFN, G_FIN, PSC, ICN = 0, KC, 2 * KC, 3 * KC, 3 * KC + AC

    banks = []
    for i in range(8):
        h = nc.alloc_psum_tensor(f"ps{i}", [P, TB], F32)
        banks.append(sc.add_tile(Tile(f"ps{i}", f"ps{i}", 0, 1, h[:, :])))
    bank_rr = [0]

    reserved = set()

    def next_bank():
        while True:
            b = banks[bank_rr[0] % 8]
            bank_rr[0] += 1
            if b.name not in reserved:
                return b

    st_c1, st_c2, st_c3 = sc.dma_stream("c1"), sc.dma_stream("c2"), sc.dma_stream("c3")
    st_x = [sc.dma_stream(f"x{i}") for i in range(4)]
    st_w = [sc.dma_stream(f"w{i}") for i in range(NSLOT)]
    st_o = [sc.dma_stream(f"o{i}") for i in range(4)]
    st_bt = [sc.dma_stream(f"bt{i}") for i in range(2)]
    st_pf = [sc.dma_stream(f"pf{i}") for i in range(2)]
    st_ebs = [sc.dma_stream(f"ebs{i}") for i in range(2)]
    st_ebl = [sc.dma_stream(f"ebl{i}") for i in range(3)]
    eb_dram = [sc.add_tile(Tile(f"ebd{h}", f"dram_eb{h}", 0, 1, None)) for h in range(NH)]

    def dma(issuer, stream, out_ap, in_ap, reads=(), writes=()):
        sc.op(issuer, lambda e, o=out_ap, i=in_ap: e.dma_start(out=o, in_=i), reads, writes, dma=stream)

    def mm(bank, cols, lhsT, rhs, start, stop, reads):
        c0, c1 = cols
        sc.op("pe", lambda e, o=bank.ap[:, c0:c1], l=lhsT, r=rhs, s=start, t=stop:
              e.matmul(o, l, r, start=s, stop=t), reads, [bank])

    def mm_rows(bank, rows, cols, lhsT, rhs, start, stop, reads):
        c0, c1 = cols
        sc.op("pe", lambda e, o=bank.ap[rows[0]:rows[1], c0:c1], l=lhsT, r=rhs, s=start, t=stop:
              e.matmul(o, l, r, start=s, stop=t), reads, [bank])

    def act(out_ap, in_ap, func, reads, writes, scale=1.0, bias=0.0):
        sc.op("act", lambda e, o=out_ap, i=in_ap, f=func, s=scale, b=bias:
              e.activation(out=o, in_=i, func=f, bias=b, scale=s), reads, writes)

    def dve_tt(out_ap, in0, in1, op, reads, writes):
        sc.op("dve", lambda e, o=out_ap, a=in0, b=in1, p=op: e.tensor_tensor(out=o, in0=a, in1=b, op=p),
              reads, writes)

    def dve_ts(out_ap, in0, s1, op0, reads, writes):
        sc.op("dve", lambda e, o=out_ap, a=in0, s=s1, p=op0:
              e.tensor_scalar(out=o, in0=a, scalar1=s, scalar2=None, op0=p), reads, writes)

    def dve_stt(out_ap, in0, scalar, in1, op0, op1, reads, writes):
        sc.op("dve", lambda e, o=out_ap, a=in0, s=scalar, b=in1, p0=op0, p1=op1:
              e.scalar_tensor_tensor(out=o, in0=a, scalar=s, in1=b, op0=p0, op1=p1), reads, writes)

    def dve_copy(out_ap, in_ap, reads, writes):
        sc.op("dve", lambda e, o=out_ap, i=in_ap: e.tensor_copy(out=o, in_=i), reads, writes)

    wcount = [0]

    class WT:
        def __init__(self, s, nk):
            self.s, self.nk = s, nk

        def lhsT(self, kc, c0, c1):
            return wslot_ap[self.s][:, kc * NW + c0: kc * NW + c1]

        def tile(self, kc):
            return wslot[self.s][0 if kc < 8 else 1]

    def wload(w_d, ncols, k0, nk, n0, nw=NW):
        s = wcount[0] % NSLOT
        wcount[0] += 1
        for h0 in range(0, nk, 8):
            n = min(8, nk - h0)
            src = bass.AP(w_d.tensor, (k0 + h0) * P * ncols + n0, [[ncols, P], [P * ncols, n], [1, nw]])
            dst = wslot_ap[s][:, h0 * NW:(h0 + n) * NW].rearrange("p (k n) -> p k n", n=NW)
            if nw != NW:
                dst = dst[:, :, 0:nw]
            dma("pool", st_w[s], dst, src, writes=[wslot[s][h0 // 8]])
        last = (st_w[s], st_w[s].n - 1)
        for h0 in range(0, nk, 8):
            wslot[s][h0 // 8].w = last
        return WT(s, nk)

    dma("pool", st_c1, ident.ap, ident_d[:, :], writes=[ident])
    dma("sp", st_c2, vecs.ap, vecs_d[:, :], writes=[vecs])
    dma("sp", st_c3, mask_t.ap, mask_d[:, :], writes=[mask_t])
    sc.op("dve", lambda e: e.memset(ones.ap, 1.0), [], [ones])
    for c in range(AC):
        sc.op("dve", lambda e, c=c: e.memset(halo[c].ap, 0.0), [], [halo[c]])
    def eb_init(h):
        i = h % 2
        dma("sp", st_bt[i], btst[i].ap, bt_d[:, h, :], writes=[btst[i]])
        dve_tt(ebst[i].ap, btst[i].ap, mask_t.ap, ALU.add, [btst[i], mask_t], [ebst[i]])
        dma("sp", st_ebs[i], eb_d[h, :, :], ebst[i].ap, reads=[ebst[i]], writes=[eb_dram[h]])

    def v_init():
        for s_ in range(2):
            for t_ in range(4):
                vv = arena[:, Vt[s_][t_][0].lo // 2: Vt[s_][t_][NH - 1].hi // 2]
                sc.op("dve", lambda e, v=vv: e.memset(v, 1.0), [], Vt[s_][t_])

    def gcol(base, k):
        return vecs.ap[:, base + k: base + k + 1]

    def rmsnorm(gbase, dst_fn):
        bank = next_bank()
        for k in range(KC):
            s = sq[k % 2]
            act(s.ap, xT[k].ap, AF.Square, [xT[k]], [s])
            mm(bank, (0, TB), ones.ap, s.ap, k == 0, k == KC - 1, [ones, s])
        act(rt.ap, bank.ap, AF.Sqrt, [bank], [rt], scale=1.0 / D, bias=EPS)
        sc.op("dve", lambda e: e.reciprocal(out=rstd.ap, in_=rt.ap), [rt], [rstd])
        for k in range(KC):
            dst_fn(k)

    ocount = [0]
    for b in range(NB):
        sl = b % 2
        c0 = b * TB
        for i in range(4):
            src = bass.AP(xT_d.tensor, (4 * i * P) * S + c0, [[S, P], [P * S, 4], [1, TB]])
            dst = arena[:, xT[4 * i].lo // 2: xT[4 * i + 3].hi // 2].bitcast(F32).rearrange(
                "p (k n) -> p k n", n=TB)
            dma("sp", st_x[i], dst, src, writes=xT[4 * i:4 * i + 4])

        def norm1_dst(k):
            dve_stt(hT[k].ap, xT[k].ap, gcol(G_MIX, k), rstd.ap, ALU.mult, ALU.mult,
                    [xT[k], vecs, rstd], [hT[k]])
        if b == 0:
            rmsnorm(G_MIX, norm1_dst)
            v_init()

        for t in range(16):
            w = wload(w_in_d, INW, 0, KC, t * NW)
            kind = t // 4
            if b == 0:
                eb_init(t)
            if kind == 2:
                for tt in range(4):
                    bank = next_bank()
                    for k in range(KC):
                        mm(bank, (0, NW), hT[k].ap[:, tt * P:(tt + 1) * P], w.lhsT(k, 0, NW),
                           k == 0, k == KC - 1, [hT[k], w.tile(k)])
                    for hh in range(4):
                        hd = (t % 4) * 4 + hh
                        cc = (hd % 2) * 64
                        act(Vt[sl][tt][hd].ap[:, cc:cc + 64], bank.ap[:, hh * 64:(hh + 1) * 64], AF.Identity,
                            [bank], [Vt[sl][tt][hd]])
                continue
            for oc in range(2):
                c = (t % 4) * 2 + oc
                bank = next_bank()
                for k in range(KC):
                    mm(bank, (0, TB), w.lhsT(k, oc * P, (oc + 1) * P), hT[k].ap,
                       k == 0, k == KC - 1, [hT[k], w.tile(k)])
                if kind == 0:
                    act(qT[c].ap, bank.ap, AF.Identity, [bank], [qT[c]], scale=0.125)
                elif kind == 1:
                    dve_copy(kT[sl][c].ap, bank.ap, [bank], [kT[sl][c]])
                else:
                    g = c // 2
                    wdw = 2 ** (g + 1)
                    dve_copy(ub.ap[:, 0:16], halo[c].ap, [halo[c]], [ub])
                    dve_copy(ub.ap[:, 16:16 + TB], bank.ap, [bank], [ub])
                    dve_copy(halo[c].ap, ub.ap[:, TB:TB + 16], [ub], [halo[c]])
                    L = 16 + TB
                    dve_tt(sA.ap[:, 1:L], ub.ap[:, 1:L], ub.ap[:, 0:L - 1], ALU.add, [ub], [sA])
                    last = sA
                    if wdw >= 4:
                        dve_tt(sB.ap[:, 3:L], sA.ap[:, 3:L], sA.ap[:, 1:L - 2], ALU.add, [sA], [sB])
                        last = sB
                    if wdw >= 8:
                        dve_tt(sA.ap[:, 7:L], sB.ap[:, 7:L], sB.ap[:, 3:L - 4], ALU.add, [sB], [sA])
                        last = sA
                    if wdw >= 16:
                        dve_tt(sB.ap[:, 15:L], sA.ap[:, 15:L], sA.ap[:, 7:L - 8], ALU.add, [sA], [sB])
                        last = sB
                    dve_stt(dT[c].ap, last.ap[:, 16:L], 1.0 / wdw, ub.ap[:, 16:L], ALU.mult, ALU.subtract,
                            [last, ub], [dT[c]])
                    if b == 0:
                        dve_tt(tiny.ap, last.ap[:, 16:32], vecs.ap[:, ICN + 16 * g: ICN + 16 * g + 16],
                               ALU.mult, [last, vecs], [tiny])
                        dve_tt(dT[c].ap[:, 0:16], tiny.ap, ub.ap[:, 16:32], ALU.subtract, [tiny, ub], [dT[c]])

        steps = []
        for h in range(NH):
            order = [4 * b, 4 * b - 1, 4 * b - 2, 4 * b - 3, 4 * b - 4, 4 * b + 1, 4 * b + 2, 4 * b + 3]
            order = [kt for kt in order if kt >= 0]
            for n_i, kt in enumerate(order):
                steps.append((h, kt, n_i == 0, n_i == len(order) - 1))
        Sb = banks[0:6]

        def geom(h, kt):
            j, half = h // 2, h % 2
            r0, r1_ = 64 * half, 64 * half + 64
            off = TB * b - P * kt
            f_lo, f_hi = max(off, 0), min(off + TB, BTW)
            return j, r0, r1_, f_lo, f_hi, f_lo - off, f_hi - off, (kt // 4) % 2

        def eb_load(h):
            dma("sp", st_ebl[h % 3], EBt[h % 3].ap, eb_d[h, :, :], reads=[eb_dram[h]], writes=[EBt[h % 3]])

        eb_load(0)
        eb_load(1)
        GS = 2
        LG = LAG // GS
        ngr = len(steps) // GS
        assert len(steps) % GS == 0
        for gi in range(ngr + LG):
            if gi < ngr:
                for it in range(gi * GS, (gi + 1) * GS):
                    h, kt, first, lastk = steps[it]
                    if first and h + 2 < NH:
                        eb_load(h + 2)
                    j, r0, r1_, f_lo, f_hi, q_lo, q_hi, ksl = geom(h, kt)
                    kc_ = (kt % 4) * P
                    sb_, pb_ = Sb[it % 6], pB[it % NPB]
                    eb = EBt[h % 3]
                    mm(sb_, (q_lo, q_hi), kT[ksl][j].ap[r0:r1_, kc_:kc_ + P], qT[j].ap[r0:r1_, q_lo:q_hi],
                       True, False, [kT[ksl][j], qT[j]])
                    mm(sb_, (q_lo, q_hi), ident.ap, eb.ap[:, f_lo:f_hi], False, True, [ident, eb])
                for it in range(gi * GS, (gi + 1) * GS):
                    h, kt, first, lastk = steps[it]
                    j, r0, r1_, f_lo, f_hi, q_lo, q_hi, ksl = geom(h, kt)
                    sb_, pb_ = Sb[it % 6], pB[it % NPB]
                    act(pb_.ap[:, q_lo:q_hi], sb_.ap[:, q_lo:q_hi], AF.Exp, [sb_], [pb_])
            g2 = gi - LG
            if g2 >= 0:
                for i2 in range(g2 * GS, (g2 + 1) * GS):
                    h, kt, first, lastk = steps[i2]
                    j, r0, r1_, f_lo, f_hi, q_lo, q_hi, ksl = geom(h, kt)
                    pb_ = pB[i2 % NPB]
                    Ob = banks[6 + (h % 2)]
                    vtile = Vt[ksl][kt % 4][h]
                    d0, d1 = (64, 128) if h % 2 == 0 else (0, 64)
                    mm(Ob, (q_lo, q_hi), vtile.ap, pb_.ap[:, q_lo:q_hi], first, lastk,
                       [vtile, pb_])
                    if lastk:
                        rc = rec[0]
                        sc.op("dve", lambda e, o=rc.ap[r0:r1_, :], i=Ob.ap[d0:d1, :]: e.reciprocal(out=o, in_=i),
                              [Ob], [rc])
                        dve_tt(yaT[j].ap[r0:r1_, :], Ob.ap[r0:r1_, :], rc.ap[r0:r1_, :], ALU.mult,
                               [Ob, rc], [yaT[j]])

        for g in range(4):
            w = wload(pool_w_d, 256, 2 * g, 2, 0)
            for ec in range(2):
                c = 2 * g + ec
                bank = next_bank()
                for k in range(2):
                    mm(bank, (0, TB), w.lhsT(k, ec * P, (ec + 1) * P), dT[2 * g + k].ap,
                       k == 0, k == 1, [dT[2 * g + k], w.tile(k)])
                dve_ts(ybT[c].ap, bank.ap, gcol(PSC, c), ALU.mult, [bank, vecs], [ybT[c]])

        for t in range(8):
            wga = wload(w_in_d, INW, 0, KC, A1 + t * NW)
            wgb = wload(w_in_d, INW, 0, KC, A2 + t * NW)
            wpa = wload(wa_d, D, 0, AC, t * NW)
            wpb = wload(wb_d, D, 0, AC, t * NW)
            for (wt, src, nk, kind) in ((wga, hT, KC, 0), (wgb, hT, KC, 1), (wpa, yaT, AC, 2), (wpb, ybT, AC, 3)):
                for oc in range(2):
                    n = 2 * t + oc
                    cs = (oc * P, (oc + 1) * P)
                    bk = next_bank()
                    for k in range(nk):
                        mm(bk, (0, TB), wt.lhsT(k, *cs), src[k].ap, k == 0, k == nk - 1, [src[k], wt.tile(k)])
                    if kind == 0:
                        act(sgA[oc].ap, bk.ap, AF.Sigmoid, [bk], [sgA[oc]])
                    elif kind == 1:
                        act(sgB[oc].ap, bk.ap, AF.Sigmoid, [bk], [sgB[oc]])
                    elif kind == 2:
                        dve_tt(sgA[oc].ap, bk.ap, sgA[oc].ap, ALU.mult, [bk, sgA[oc]], [sgA[oc]])
                    else:
                        dve_tt(sgB[oc].ap, bk.ap, sgB[oc].ap, ALU.mult, [bk, sgB[oc]], [sgB[oc]])
                        dve_tt(mT[n].ap, sgA[oc].ap, sgB[oc].ap, ALU.add, [sgA[oc], sgB[oc]], [mT[n]])

        for t in range(8):
            w = wload(wo_d, D, 0, KC, t * NW)
            for oc in range(2):
                n = 2 * t + oc
                bank = next_bank()
                for k in range(KC):
                    mm(bank, (0, TB), w.lhsT(k, oc * P, (oc + 1) * P), mT[k].ap,
                       k == 0, k == KC - 1, [mT[k], w.tile(k)])
                dve_tt(xT[n].ap, bank.ap, xT[n].ap, ALU.add, [bank, xT[n]], [xT[n]])

        def norm2_dst(k):
            dve_stt(hT[k].ap, xT[k].ap, gcol(G_FFN, k), rstd.ap, ALU.mult, ALU.mult,
                    [xT[k], vecs, rstd], [hT[k]])
        rmsnorm(G_FFN, norm2_dst)

        for hf in range(2):
            for tt in range(FH // 2):
                tg = hf * (FH // 2) + tt
                wg = wload(wgu_d, 2 * DFF, 0, KC, tg * NW)
                wu = wload(wgu_d, 2 * DFF, 0, KC, DFF + tg * NW)
                for oc in range(2):
                    i = 2 * tt + oc
                    cs = (oc * P, (oc + 1) * P)
                    bg, bu = next_bank(), next_bank()
                    for k in range(KC):
                        mm(bg, (0, TB), wg.lhsT(k, *cs), hT[k].ap, k == 0, k == KC - 1, [hT[k], wg.tile(k)])
                    for k in range(KC):
                        mm(bu, (0, TB), wu.lhsT(k, *cs), hT[k].ap, k == 0, k == KC - 1, [hT[k], wu.tile(k)])
                    sg = sgA[i % 2]
                    act(sg.ap, bg.ap, AF.Silu, [bg], [sg])
                    dve_tt(aT[i].ap, bu.ap, sg.ap, ALU.mult, [bu, sg], [aT[i]])
            pre = (hf == 1 and b + 1 < NB)
            if pre:
                pf_bank = next_bank()
                reserved.add(pf_bank.name)

            def pf_dma(item):
                e = item % 8
                src = bass.AP(xT_d.tensor, (2 * e * P) * S + (b + 1) * TB, [[S, P], [P * S, 2], [1, TB]])
                dst = stg[e % 2].ap.rearrange("p (k n) -> p k n", n=TB)
                dma("sp", st_pf[e % 2], dst, src, writes=[stg[e % 2]])

            def pf_norm(e):
                for kk in range(2):
                    k = 2 * e + kk
                    dve_stt(hT[k].ap, stg[e % 2].ap[:, kk * TB:(kk + 1) * TB], gcol(G_MIX, k), rstd_n.ap,
                            ALU.mult, ALU.mult, [stg[e % 2], vecs, rstd_n], [hT[k]])

            for t in range(8):
                wd0 = wload(wdn_d, D, hf * FH, 11, t * NW)
                wd1 = wload(wdn_d, D, hf * FH + 11, 11, t * NW)
                for oc in range(2):
                    n = 2 * t + oc
                    g = n
                    if pre:
                        if 1 <= g <= 8:
                            for kk in range(2):
                                sqt = sqn[(2 * (g - 1) + kk) % 4]
                                act(sqt.ap, stg[(g - 1) % 2].ap[:, kk * TB:(kk + 1) * TB], AF.Square,
                                    [stg[(g - 1) % 2]], [sqt])
                        if g >= 10:
                            pf_norm(g - 10)
                        pf_dma(g)
                    cs = (oc * P, (oc + 1) * P)
                    bank = next_bank()
                    for i in range(FH):
                        w = wd0 if i < 11 else wd1
                        k = i if i < 11 else i - 11
                        mm(bank, (0, TB), w.lhsT(k, *cs), aT[i].ap, i == 0, i == FH - 1, [aT[i], w.tile(k)])
                    dve_tt(xT[n].ap, bank.ap, xT[n].ap, ALU.add, [bank, xT[n]], [xT[n]])
                    if pre and 2 <= g <= 9:
                        e = g - 2
                        for kk in range(2):
                            sqt = sqn[(2 * e + kk) % 4]
                            mm(pf_bank, (0, TB), ones.ap, sqt.ap, e == 0 and kk == 0, e == 7 and kk == 1,
                               [ones, sqt])
                        if g == 9:
                            act(rt.ap, pf_bank.ap, AF.Sqrt, [pf_bank], [rt], scale=1.0 / D, bias=EPS)
                            sc.op("dve", lambda e_: e_.reciprocal(out=rstd_n.ap, in_=rt.ap), [rt], [rstd_n])
                            reserved.discard(pf_bank.name)
            if pre:
                pf_norm(6)
                pf_norm(7)

        def fin_dst(k):
            o = (sgA + sgB)[ocount[0] % 4]
            st = st_o[ocount[0] % 4]
            ocount[0] += 1
            dve_stt(o.ap, xT[k].ap, gcol(G_FIN, k), rstd.ap, ALU.mult, ALU.mult, [xT[k], vecs, rstd], [o])
            dma("sp", st, yT_d[k * P:(k + 1) * P, c0:c0 + TB], o.ap, reads=[o])
        rmsnorm(G_FIN, fin_dst)

    for st in st_o:
        if st.n > 0:
            sc.final_wait("sp", (st, st.n - 1))

    sc.finalize()
    with nc.allow_low_precision("bf16 matmul operands, fp32 accumulation"):
        with nc.Block() as block:
            @block.tensor
            def _(e):
                sc.replay("pe", e)

            @block.scalar
            def _(e):
                sc.replay("act", e)

            @block.vector
            def _(e):
                sc.replay("dve", e)

            @block.gpsimd
            def _(e):
                sc.replay("pool", e)

            @block.sync
            def _(e):
                sc.replay("sp", e)
    return nc


def _consts():
    p = np.arange(P)[:, None]
    f = np.arange(BTW)[None, :]
    dist = f - p
    idx = np.clip(dist, -128, 128) + 128
    vis = ((p // 64) <= (f // 64)) & ((f // 64) <= (p // 64) + 8)
    mask = np.where(vis, 0.0, MASKV).astype(np.float32)
    ident = np.eye(P, dtype=np.float32)
    invc = np.zeros((4, 16), np.float32)
    for g, w in enumerate((2, 4, 8, 16)):
        invc[g] = 1.0 / np.minimum(np.arange(16) + 1, w)
    return idx, mask, ident, invc


def _prep_shared(norm_mix, w_in, rel_bias, pool_w, pool_scale, w_branch_a, w_branch_b, w_out,
                 norm_ffn, w_gate_up, w_down, norm_final):
    idx, mask, ident, invc = _consts()
    f32 = np.float32
    bt = np.ascontiguousarray(np.transpose(np.asarray(rel_bias[0], f32)[:, idx], (1, 0, 2)))

    def cols(v, n):
        return np.asarray(v, f32).reshape(n, P).T

    vecs = np.concatenate([cols(norm_mix[0], KC), cols(norm_ffn[0], KC), cols(norm_final, KC),
                           cols(pool_scale[0], AC), np.broadcast_to(invc.reshape(1, 64), (P, 64))], axis=1)
    return {
        "w_in": np.ascontiguousarray(np.asarray(w_in[0], f32)),
        "pool_w": np.ascontiguousarray(np.asarray(pool_w[0], f32).reshape(AW, 256)),
        "w_a": np.ascontiguousarray(np.asarray(w_branch_a[0], f32)),
        "w_b": np.ascontiguousarray(np.asarray(w_branch_b[0], f32)),
        "w_o": np.ascontiguousarray(np.asarray(w_out[0], f32)),
        "w_gu": np.ascontiguousarray(np.asarray(w_gate_up[0], f32)),
        "w_dn": np.ascontiguousarray(np.asarray(w_down[0], f32)),
        "bt": bt, "mask": mask, "ident": ident,
        "vecs": np.ascontiguousarray(vecs.astype(f32)),
    }


_NC_CACHE = {}


def kernel(x, norm_mix, w_in, rel_bias, pool_w, pool_scale, w_branch_a, w_branch_b,
           w_out, norm_ffn, w_gate_up, w_down, norm_final):
    x = np.asarray(x, np.float32)
    B, S, _ = x.shape
    shared = _prep_shared(norm_mix, w_in, rel_bias, pool_w, pool_scale, w_branch_a, w_branch_b,
                          w_out, norm_ffn, w_gate_up, w_down, norm_final)
    if S not in _NC_CACHE:
        _NC_CACHE[S] = build_program(S)
    nc = _NC_CACHE[S]
    in_maps = []
    for c in range(B):
        m = dict(shared)
        m["xT"] = np.ascontiguousarray(x[c].T)
        in_maps.append(m)
    res = run_bass_kernel_spmd(nc, in_maps, core_ids=list(range(B)))
    out = np.stack([np.ascontiguousarray(r["yT"].T) for r in res.results], axis=0)
    return out.astype(np.float32)
```

```python
import numpy as np
import concourse.bass as bass
import concourse.mybir as mybir
from concourse.bass_utils import run_bass_kernel_spmd

F32 = mybir.dt.float32
BF16 = mybir.dt.bfloat16
AF = mybir.ActivationFunctionType
ALU = mybir.AluOpType

P = 128
D = 2048
KC = D // P
TB = 512
AW = 1024
AC = AW // P
NH = 16
DFF = 5632
FC = DFF // P
FH = FC // 2
INW = 8192
NW = 256
NSLOT = 5
EPS = 1e-6
A0 = 3 * AW
A1 = A0 + AW
A2 = A1 + D
BTW = 640
MASKV = -30000.0
SAME_ENGINE_SYNC = True


class Stream:
    def __init__(self, name, sem, step):
        self.name, self.sem, self.step = name, sem, step
        self.n = 0
        self.observed = set()
        self.rank = None
        self.snap = []

    def value(self, idx):
        if self.step == 16:
            return 16 * (idx + 1)
        return self.rank[idx]


class Tile:
    def __init__(self, name, space, lo, hi, ap):
        self.name, self.space, self.lo, self.hi = name, space, lo, hi
        self.ap = ap
        self.w = None
        self.r = {}
        self.over = []

    def __getitem__(self, key):
        return self.ap[key]


class Op:
    __slots__ = ("stream", "idx", "fn", "waits", "is_dma")

    def __init__(self, stream, idx, fn, waits, is_dma):
        self.stream, self.idx, self.fn, self.waits, self.is_dma = stream, idx, fn, waits, is_dma


class Sched:
    ENG = ("pe", "act", "dve", "pool", "sp")

    def __init__(self, nc):
        self.nc = nc
        self.streams = {}
        for e in ("pe", "act", "dve", "pool"):
            self.streams[e] = Stream(e, nc.alloc_semaphore("s_" + e), 1)
        self.prog = {e: [] for e in self.ENG}
        self.known = {e: {} for e in self.ENG}
        self.tiles = []

    def dma_stream(self, name):
        st = Stream(name, self.nc.alloc_semaphore("d_" + name), 16)
        self.streams[name] = st
        return st

    def add_tile(self, t):
        for o in self.tiles:
            if o.space == t.space and o.lo < t.hi and t.lo < o.hi:
                o.over.append(t)
                t.over.append(o)
        self.tiles.append(t)
        return t

    def op(self, issuer, fn, reads=(), writes=(), dma=None):
        st = self.streams[issuer] if dma is None else dma
        idx = st.n
        st.n += 1
        deps = {}
        raw = {}

        def need(si, d):
            s, i = si
            if d.get(s, -1) < i:
                d[s] = i

        for t in reads:
            if t.w is not None:
                need(t.w, raw)
            for o in t.over:
                if o.w is not None:
                    need(o.w, raw)
        for t in writes:
            for o in [t] + t.over:
                if o.w is not None:
                    need(o.w, deps)
                for s, i in o.r.items():
                    need((s, i), deps)
        kn = self.known[issuer]
        waits = []
        for d, is_raw in ((raw, True), (deps, False)):
            for s, i in d.items():
                if dma is None and s is st:
                    if issuer == "pe" or not SAME_ENGINE_SYNC or not is_raw:
                        continue
                if kn.get(s, -1) >= i:
                    continue
                kn[s] = i
                waits.append((s, i))
                s.observed.add(i)
                for s2, i2 in s.snap[i].items():
                    if kn.get(s2, -1) < i2:
                        kn[s2] = i2
        st.snap.append(dict(kn))
        for t in reads:
            if t.r.get(st, -1) < idx:
                t.r[st] = idx
        for t in writes:
            t.w = (st, idx)
            t.r = {}
        self.prog[issuer].append(Op(st, idx, fn, waits, dma is not None))
        return (st, idx)

    def final_wait(self, issuer, st_idx):
        s, i = st_idx
        s.observed.add(i)
        self.prog[issuer].append(Op(None, None, None, [(s, i)], False))

    def finalize(self):
        for s in self.streams.values():
            if s.step == 1:
                s.rank = {}
                for r, i in enumerate(sorted(s.observed)):
                    s.rank[i] = r + 1

    def replay(self, issuer, eng):
        for o in self.prog[issuer]:
            for s, i in o.waits:
                eng.wait_ge(s.sem, s.value(i))
            if o.fn is None:
                continue
            ins = o.fn(eng)
            if o.is_dma:
                ins.then_inc(o.stream.sem, 16)
            elif o.idx in o.stream.observed:
                ins.then_inc(o.stream.sem, 1)


def build_program(S, debug=False):
    NB = S // TB
    nc = bass.Bass("TRN2", target_bir_lowering=False)
    sc = Sched(nc)

    def dram_in(name, shape):
        return nc.dram_tensor(name, list(shape), F32, kind="ExternalInput").ap()

    xT_d = dram_in("xT", [D, S])
    w_in_d = dram_in("w_in", [D, INW])
    pool_w_d = dram_in("pool_w", [AW, 256])
    wa_d = dram_in("w_a", [AW, D])
    wb_d = dram_in("w_b", [AW, D])
    wo_d = dram_in("w_o", [D, D])
    wgu_d = dram_in("w_gu", [D, 2 * DFF])
    wdn_d = dram_in("w_dn", [DFF, D])
    bt_d = dram_in("bt", [P, NH, BTW])
    mask_d = dram_in("mask", [P, BTW])
    ident_d = dram_in("ident", [P, P])
    vecs_d = dram_in("vecs", [P, 3 * KC + AC + 64])
    yT_d = nc.dram_tensor("yT", [D, S], F32, kind="ExternalOutput").ap()
    eb_d = nc.dram_tensor("eb_scr", [NH, P, BTW], BF16, kind="Internal").ap()

    ARENA = 207 * 1024
    arena = nc.alloc_sbuf_tensor("arena", [P, ARENA // 2], BF16)
    cur = [0]

    def mk(name, nbytes, dtype, at=None):
        off = cur[0] if at is None else at
        assert off % 4 == 0 and nbytes % 4 == 0
        if at is None:
            cur[0] += nbytes
            assert cur[0] <= ARENA, (name, cur[0])
        ap = arena[:, off // 2:(off + nbytes) // 2]
        if dtype == F32:
            ap = ap.bitcast(F32)
        return sc.add_tile(Tile(name, "sb", off, off + nbytes, ap))

    xT = [mk(f"xT{k}", TB * 4, F32) for k in range(KC)]
    hT = [mk(f"hT{k}", TB * 2, BF16) for k in range(KC)]
    r1 = cur[0]
    qT = [mk(f"qT{c}", TB * 2, BF16) for c in range(AC)]
    dT = [mk(f"dT{c}", TB * 2, BF16) for c in range(AC)]
    yaT = [mk(f"yaT{c}", TB * 2, BF16) for c in range(AC)]
    ybT = [mk(f"ybT{c}", TB * 2, BF16) for c in range(AC)]
    r1_end = cur[0]
    mT = [mk(f"mT{k}", TB * 2, BF16, at=r1 + k * TB * 2) for k in range(KC)]
    aT = [mk(f"aT{i}", TB * 2, BF16, at=r1 + i * TB * 2) for i in range(FH)]
    assert r1 + FH * TB * 2 <= r1_end
    ri = r1 + 16 * 1024
    mask_t = mk("mask", BTW * 4, F32, at=ri)
    btst = [mk(f"btst{i}", BTW * 4, F32, at=ri + (i + 1) * BTW * 4) for i in range(2)]
    ebst = [mk(f"ebst{i}", BTW * 2, BF16, at=ri + 3 * BTW * 4 + i * BTW * 2) for i in range(2)]
    kT = [[mk(f"kT{s}_{c}", TB * 2, BF16) for c in range(AC)] for s in range(2)]
    Vt = [[[mk(f"V{s}_{t}_{h}", P * 2, BF16) for h in range(NH)] for t in range(4)] for s in range(2)]
    EBt = [mk(f"EB{i}", BTW * 2, BF16) for i in range(3)]
    wslot = [[mk(f"w{s}_{h}", 8 * NW * 2, BF16) for h in range(2)] for s in range(NSLOT)]
    wslot_ap = [arena[:, wslot[s][0].lo // 2: wslot[s][1].hi // 2] for s in range(NSLOT)]
    halo = [mk(f"halo{c}", 16 * 4, F32) for c in range(AC)]
    LAG = 4
    NPB = LAG + 2
    pB = [mk(f"pB{i}", TB * 2, BF16) for i in range(NPB)]
    ident = mk("ident", P * 2, BF16)
    rec = [mk(f"rec{i}", TB * 4, F32) for i in range(1)]
    sq = [mk(f"sq{i}", TB * 2, BF16) for i in range(2)]
    rstd = mk("rstd", TB * 4, F32)
    rt = rstd
    sgA = [mk(f"sgA{i}", TB * 4, F32) for i in range(2)]
    sgB = [mk(f"sgB{i}", TB * 4, F32) for i in range(2)]
    ub = mk("ub", (16 + TB) * 4, F32, at=sgA[0].lo)
    sA = mk("sA", (16 + TB) * 4, F32, at=sgA[0].lo + (16 + TB) * 4)
    sB = mk("sB", (16 + TB) * 4, F32, at=sgA[0].lo + 2 * (16 + TB) * 4)
    assert sgA[0].lo + 3 * (16 + TB) * 4 <= sgB[1].hi
    rstd_n = mk("rstd_n", TB * 4, F32, at=pB[0].lo)
    sqn = [mk(f"sqn{i}", TB * 2, BF16, at=pB[0].lo + TB * 4 + i * TB * 2) for i in range(4)]
    assert pB[0].lo + TB * 4 + 4 * TB * 2 <= pB[NPB - 1].hi
    stg = [mk(f"stg{i}", 2 * TB * 4, F32, at=r1 + 24 * 1024 + i * 2 * TB * 4) for i in range(2)]
    ostg = [mk(f"ostg{i}", TB * 4, F32) for i in range(2)]
    ones = mk("ones", P * 2, BF16)
    vecs = mk("vecs", (3 * KC + AC + 64) * 4, F32)
    tiny = mk("tiny", 16 * 4, F32)
    G_MIX, G_FFN, G_FIN, PSC, ICN = 0, KC, 2 * KC, 3 * KC, 3 * KC + AC

    banks = []
    for i in range(8):
        h = nc.alloc_psum_tensor(f"ps{i}", [P, TB], F32)
        banks.append(sc.add_tile(Tile(f"ps{i}", f"ps{i}", 0, 1, h[:, :])))
    bank_rr = [0]

    reserved = set()

    def next_bank():
        while True:
            b = banks[bank_rr[0] % 8]
            bank_rr[0] += 1
            if b.name not in reserved:
                return b

    st_c1, st_c2, st_c3 = sc.dma_stream("c1"), sc.dma_stream("c2"), sc.dma_stream("c3")
    st_x = [sc.dma_stream(f"x{i}") for i in range(4)]
    st_w = [sc.dma_stream(f"w{i}") for i in range(NSLOT)]
    st_o = [sc.dma_stream(f"o{i}") for i in range(6)]
    st_bt = [sc.dma_stream(f"bt{i}") for i in range(2)]
    st_pf = [sc.dma_stream(f"pf{i}") for i in range(2)]
    st_ebs = [sc.dma_stream(f"ebs{i}") for i in range(2)]
    st_ebl = [sc.dma_stream(f"ebl{i}") for i in range(3)]
    eb_dram = [sc.add_tile(Tile(f"ebd{h}", f"dram_eb{h}", 0, 1, None)) for h in range(NH)]

    def dma(issuer, stream, out_ap, in_ap, reads=(), writes=()):
        sc.op(issuer, lambda e, o=out_ap, i=in_ap: e.dma_start(out=o, in_=i), reads, writes, dma=stream)

    def mm(bank, cols, lhsT, rhs, start, stop, reads):
        c0, c1 = cols
        sc.op("pe", lambda e, o=bank.ap[:, c0:c1], l=lhsT, r=rhs, s=start, t=stop:
              e.matmul(o, l, r, start=s, stop=t), reads, [bank])

    def mm_rows(bank, rows, cols, lhsT, rhs, start, stop, reads):
        c0, c1 = cols
        sc.op("pe", lambda e, o=bank.ap[rows[0]:rows[1], c0:c1], l=lhsT, r=rhs, s=start, t=stop:
              e.matmul(o, l, r, start=s, stop=t), reads, [bank])

    def act(out_ap, in_ap, func, reads, writes, scale=1.0, bias=0.0):
        sc.op("act", lambda e, o=out_ap, i=in_ap, f=func, s=scale, b=bias:
              e.activation(out=o, in_=i, func=f, bias=b, scale=s), reads, writes)

    def dve_tt(out_ap, in0, in1, op, reads, writes):
        sc.op("dve", lambda e, o=out_ap, a=in0, b=in1, p=op: e.tensor_tensor(out=o, in0=a, in1=b, op=p),
              reads, writes)

    def dve_ts(out_ap, in0, s1, op0, reads, writes):
        sc.op("dve", lambda e, o=out_ap, a=in0, s=s1, p=op0:
              e.tensor_scalar(out=o, in0=a, scalar1=s, scalar2=None, op0=p), reads, writes)

    def dve_stt(out_ap, in0, scalar, in1, op0, op1, reads, writes):
        sc.op("dve", lambda e, o=out_ap, a=in0, s=scalar, b=in1, p0=op0, p1=op1:
              e.scalar_tensor_tensor(out=o, in0=a, scalar=s, in1=b, op0=p0, op1=p1), reads, writes)

    def dve_copy(out_ap, in_ap, reads, writes):
        sc.op("dve", lambda e, o=out_ap, i=in_ap: e.tensor_copy(out=o, in_=i), reads, writes)

    wcount = [0]

    class WT:
        def __init__(self, s, nk):
            self.s, self.nk = s, nk

        def lhsT(self, kc, c0, c1):
            return wslot_ap[self.s][:, kc * NW + c0: kc * NW + c1]

        def tile(self, kc):
            return wslot[self.s][0 if kc < 8 else 1]

    def wload(w_d, ncols, k0, nk, n0, nw=NW):
        s = wcount[0] % NSLOT
        wcount[0] += 1
        for h0 in range(0, nk, 8):
            n = min(8, nk - h0)
            src = bass.AP(w_d.tensor, (k0 + h0) * P * ncols + n0, [[ncols, P], [P * ncols, n], [1, nw]])
            dst = wslot_ap[s][:, h0 * NW:(h0 + n) * NW].rearrange("p (k n) -> p k n", n=NW)
            if nw != NW:
                dst = dst[:, :, 0:nw]
            dma("pool", st_w[s], dst, src, writes=[wslot[s][h0 // 8]])
        last = (st_w[s], st_w[s].n - 1)
        for h0 in range(0, nk, 8):
            wslot[s][h0 // 8].w = last
        return WT(s, nk)

    dma("pool", st_c1, ident.ap, ident_d[:, :], writes=[ident])
    dma("sp", st_c2, vecs.ap, vecs_d[:, :], writes=[vecs])
    dma("sp", st_c3, mask_t.ap, mask_d[:, :], writes=[mask_t])
    sc.op("dve", lambda e: e.memset(ones.ap, 1.0), [], [ones])
    for c in range(AC):
        sc.op("dve", lambda e, c=c: e.memset(halo[c].ap, 0.0), [], [halo[c]])
    def eb_init(h):
        i = h % 2
        dma("sp", st_bt[i], btst[i].ap, bt_d[:, h, :], writes=[btst[i]])
        dve_tt(ebst[i].ap, btst[i].ap, mask_t.ap, ALU.add, [btst[i], mask_t], [ebst[i]])
        dma("sp", st_ebs[i], eb_d[h, :, :], ebst[i].ap, reads=[ebst[i]], writes=[eb_dram[h]])

    def v_init():
        for s_ in range(2):
            for t_ in range(4):
                vv = arena[:, Vt[s_][t_][0].lo // 2: Vt[s_][t_][NH - 1].hi // 2]
                sc.op("dve", lambda e, v=vv: e.memset(v, 1.0), [], Vt[s_][t_])

    def gcol(base, k):
        return vecs.ap[:, base + k: base + k + 1]

    def rmsnorm(gbase, dst_fn):
        bank = next_bank()
        for k in range(KC):
            s = sq[k % 2]
            act(s.ap, xT[k].ap, AF.Square, [xT[k]], [s])
            mm(bank, (0, TB), ones.ap, s.ap, k == 0, k == KC - 1, [ones, s])
        act(rt.ap, bank.ap, AF.Sqrt, [bank], [rt], scale=1.0 / D, bias=EPS)
        sc.op("dve", lambda e: e.reciprocal(out=rstd.ap, in_=rt.ap), [rt], [rstd])
        for k in range(KC):
            dst_fn(k)

    ocount = [0]
    for b in range(NB):
        sl = b % 2
        c0 = b * TB
        for i in range(4):
            src = bass.AP(xT_d.tensor, (4 * i * P) * S + c0, [[S, P], [P * S, 4], [1, TB]])
            dst = arena[:, xT[4 * i].lo // 2: xT[4 * i + 3].hi // 2].bitcast(F32).rearrange(
                "p (k n) -> p k n", n=TB)
            dma("sp", st_x[i], dst, src, writes=xT[4 * i:4 * i + 4])

        def norm1_dst(k):
            dve_stt(hT[k].ap, xT[k].ap, gcol(G_MIX, k), rstd.ap, ALU.mult, ALU.mult,
                    [xT[k], vecs, rstd], [hT[k]])
        if b == 0:
            v_init()
            rmsnorm(G_MIX, norm1_dst)

        for t in range(16):
            w = wload(w_in_d, INW, 0, KC, t * NW)
            kind = t // 4
            if b == 0:
                eb_init(t)
            if kind == 2:
                for tt in range(4):
                    bank = next_bank()
                    for k in range(KC):
                        mm(bank, (0, NW), hT[k].ap[:, tt * P:(tt + 1) * P], w.lhsT(k, 0, NW),
                           k == 0, k == KC - 1, [hT[k], w.tile(k)])
                    for hh in range(4):
                        hd = (t % 4) * 4 + hh
                        cc = (hd % 2) * 64
                        act(Vt[sl][tt][hd].ap[:, cc:cc + 64], bank.ap[:, hh * 64:(hh + 1) * 64], AF.Identity,
                            [bank], [Vt[sl][tt][hd]])
                continue
            for oc in range(2):
                c = (t % 4) * 2 + oc
                bank = next_bank()
                for k in range(KC):
                    mm(bank, (0, TB), w.lhsT(k, oc * P, (oc + 1) * P), hT[k].ap,
                       k == 0, k == KC - 1, [hT[k], w.tile(k)])
                if kind == 0:
                    act(qT[c].ap, bank.ap, AF.Identity, [bank], [qT[c]], scale=0.125)
                elif kind == 1:
                    dve_copy(kT[sl][c].ap, bank.ap, [bank], [kT[sl][c]])
                else:
                    g = c // 2
                    wdw = 2 ** (g + 1)
                    dve_copy(ub.ap[:, 0:16], halo[c].ap, [halo[c]], [ub])
                    dve_copy(ub.ap[:, 16:16 + TB], bank.ap, [bank], [ub])
                    dve_copy(halo[c].ap, ub.ap[:, TB:TB + 16], [ub], [halo[c]])
                    L = 16 + TB
                    dve_tt(sA.ap[:, 1:L], ub.ap[:, 1:L], ub.ap[:, 0:L - 1], ALU.add, [ub], [sA])
                    last = sA
                    if wdw >= 4:
                        dve_tt(sB.ap[:, 3:L], sA.ap[:, 3:L], sA.ap[:, 1:L - 2], ALU.add, [sA], [sB])
                        last = sB
                    if wdw >= 8:
                        dve_tt(sA.ap[:, 7:L], sB.ap[:, 7:L], sB.ap[:, 3:L - 4], ALU.add, [sB], [sA])
                        last = sA
                    if wdw >= 16:
                        dve_tt(sB.ap[:, 15:L], sA.ap[:, 15:L], sA.ap[:, 7:L - 8], ALU.add, [sA], [sB])
                        last = sB
                    dve_stt(dT[c].ap, last.ap[:, 16:L], 1.0 / wdw, ub.ap[:, 16:L], ALU.mult, ALU.subtract,
                            [last, ub], [dT[c]])
                    if b == 0:
                        dve_tt(tiny.ap, last.ap[:, 16:32], vecs.ap[:, ICN + 16 * g: ICN + 16 * g + 16],
                               ALU.mult, [last, vecs], [tiny])
                        dve_tt(dT[c].ap[:, 0:16], tiny.ap, ub.ap[:, 16:32], ALU.subtract, [tiny, ub], [dT[c]])

        steps = []
        for h in range(NH):
            order = [4 * b, 4 * b - 1, 4 * b - 2, 4 * b - 3, 4 * b - 4, 4 * b + 1, 4 * b + 2, 4 * b + 3]
            order = [kt for kt in order if kt >= 0]
            for n_i, kt in enumerate(order):
                steps.append((h, kt, n_i == 0, n_i == len(order) - 1))
        Sb = banks[0:4]

        def geom(h, kt):
            j, half = h // 2, h % 2
            r0, r1_ = 64 * half, 64 * half + 64
            off = TB * b - P * kt
            f_lo, f_hi = max(off, 0), min(off + TB, BTW)
            return j, r0, r1_, f_lo, f_hi, f_lo - off, f_hi - off, (kt // 4) % 2

        def eb_load(h):
            dma("sp", st_ebl[h % 3], EBt[h % 3].ap, eb_d[h, :, :], reads=[eb_dram[h]], writes=[EBt[h % 3]])

        eb_load(0)
        eb_load(1)
        GS = 2
        LG = LAG // GS
        ngr = len(steps) // GS
        assert len(steps) % GS == 0
        for gi in range(ngr + LG):
            if gi < ngr:
                for it in range(gi * GS, (gi + 1) * GS):
                    h, kt, first, lastk = steps[it]
                    if first and h + 2 < NH:
                        eb_load(h + 2)
                    j, r0, r1_, f_lo, f_hi, q_lo, q_hi, ksl = geom(h, kt)
                    kc_ = (kt % 4) * P
                    sb_, pb_ = Sb[it % 4], pB[it % NPB]
                    eb = EBt[h % 3]
                    mm(sb_, (q_lo, q_hi), kT[ksl][j].ap[r0:r1_, kc_:kc_ + P], qT[j].ap[r0:r1_, q_lo:q_hi],
                       True, False, [kT[ksl][j], qT[j]])
                    mm(sb_, (q_lo, q_hi), ident.ap, eb.ap[:, f_lo:f_hi], False, True, [ident, eb])
                for it in range(gi * GS, (gi + 1) * GS):
                    h, kt, first, lastk = steps[it]
                    j, r0, r1_, f_lo, f_hi, q_lo, q_hi, ksl = geom(h, kt)
                    sb_, pb_ = Sb[it % 4], pB[it % NPB]
                    act(pb_.ap[:, q_lo:q_hi], sb_.ap[:, q_lo:q_hi], AF.Exp, [sb_], [pb_])
            g2 = gi - LG
            if g2 >= 0:
                for i2 in range(g2 * GS, (g2 + 1) * GS):
                    h, kt, first, lastk = steps[i2]
                    j, r0, r1_, f_lo, f_hi, q_lo, q_hi, ksl = geom(h, kt)
                    pb_ = pB[i2 % NPB]
                    Ob = banks[4 + (h % 4)]
                    vtile = Vt[ksl][kt % 4][h]
                    d0, d1 = (64, 128) if h % 2 == 0 else (0, 64)
                    mm(Ob, (q_lo, q_hi), vtile.ap, pb_.ap[:, q_lo:q_hi], first, lastk,
                       [vtile, pb_])
                    if lastk:
                        rc = rec[0]
                        sc.op("dve", lambda e, o=rc.ap[r0:r1_, :], i=Ob.ap[d0:d1, :]: e.reciprocal(out=o, in_=i),
                              [Ob], [rc])
                        dve_tt(yaT[j].ap[r0:r1_, :], Ob.ap[r0:r1_, :], rc.ap[r0:r1_, :], ALU.mult,
                               [Ob, rc], [yaT[j]])

        for g in range(4):
            w = wload(pool_w_d, 256, 2 * g, 2, 0)
            for ec in range(2):
                c = 2 * g + ec
                bank = next_bank()
                for k in range(2):
                    mm(bank, (0, TB), w.lhsT(k, ec * P, (ec + 1) * P), dT[2 * g + k].ap,
                       k == 0, k == 1, [dT[2 * g + k], w.tile(k)])
                dve_ts(ybT[c].ap, bank.ap, gcol(PSC, c), ALU.mult, [bank, vecs], [ybT[c]])

        for t in range(8):
            wga = wload(w_in_d, INW, 0, KC, A1 + t * NW)
            wgb = wload(w_in_d, INW, 0, KC, A2 + t * NW)
            wpa = wload(wa_d, D, 0, AC, t * NW)
            wpb = wload(wb_d, D, 0, AC, t * NW)
            for (wt, src, nk, kind) in ((wga, hT, KC, 0), (wgb, hT, KC, 1), (wpa, yaT, AC, 2), (wpb, ybT, AC, 3)):
                for oc in range(2):
                    n = 2 * t + oc
                    cs = (oc * P, (oc + 1) * P)
                    bk = next_bank()
                    for k in range(nk):
                        mm(bk, (0, TB), wt.lhsT(k, *cs), src[k].ap, k == 0, k == nk - 1, [src[k], wt.tile(k)])
                    if kind == 0:
                        act(sgA[oc].ap, bk.ap, AF.Sigmoid, [bk], [sgA[oc]])
                    elif kind == 1:
                        act(sgB[oc].ap, bk.ap, AF.Sigmoid, [bk], [sgB[oc]])
                    elif kind == 2:
                        dve_tt(sgA[oc].ap, bk.ap, sgA[oc].ap, ALU.mult, [bk, sgA[oc]], [sgA[oc]])
                    else:
                        dve_tt(sgB[oc].ap, bk.ap, sgB[oc].ap, ALU.mult, [bk, sgB[oc]], [sgB[oc]])
                        dve_tt(mT[n].ap, sgA[oc].ap, sgB[oc].ap, ALU.add, [sgA[oc], sgB[oc]], [mT[n]])

        for t in range(8):
            w = wload(wo_d, D, 0, KC, t * NW)
            for oc in range(2):
                n = 2 * t + oc
                bank = next_bank()
                for k in range(KC):
                    mm(bank, (0, TB), w.lhsT(k, oc * P, (oc + 1) * P), mT[k].ap,
                       k == 0, k == KC - 1, [mT[k], w.tile(k)])
                dve_tt(xT[n].ap, bank.ap, xT[n].ap, ALU.add, [bank, xT[n]], [xT[n]])

        def norm2_dst(k):
            dve_stt(hT[k].ap, xT[k].ap, gcol(G_FFN, k), rstd.ap, ALU.mult, ALU.mult,
                    [xT[k], vecs, rstd], [hT[k]])
        rmsnorm(G_FFN, norm2_dst)

        for hf in range(2):
            for tt in range(FH // 2):
                tg = hf * (FH // 2) + tt
                wg = wload(wgu_d, 2 * DFF, 0, KC, tg * NW)
                wu = wload(wgu_d, 2 * DFF, 0, KC, DFF + tg * NW)
                for oc in range(2):
                    i = 2 * tt + oc
                    cs = (oc * P, (oc + 1) * P)
                    bg, bu = next_bank(), next_bank()
                    for k in range(KC):
                        mm(bg, (0, TB), wg.lhsT(k, *cs), hT[k].ap, k == 0, k == KC - 1, [hT[k], wg.tile(k)])
                    for k in range(KC):
                        mm(bu, (0, TB), wu.lhsT(k, *cs), hT[k].ap, k == 0, k == KC - 1, [hT[k], wu.tile(k)])
                    sg = sgA[i % 2]
                    act(sg.ap, bg.ap, AF.Silu, [bg], [sg])
                    dve_tt(aT[i].ap, bu.ap, sg.ap, ALU.mult, [bu, sg], [aT[i]])
            pre = (hf == 1 and b + 1 < NB)
            fin_ov = (hf == 1 and b + 1 == NB)
            if pre:
                pf_bank = next_bank()
                reserved.add(pf_bank.name)
            if fin_ov:
                fin_bank = next_bank()
                reserved.add(fin_bank.name)

            def pf_dma(item):
                e = item % 8
                src = bass.AP(xT_d.tensor, (2 * e * P) * S + (b + 1) * TB, [[S, P], [P * S, 2], [1, TB]])
                dst = stg[e % 2].ap.rearrange("p (k n) -> p k n", n=TB)
                dma("sp", st_pf[e % 2], dst, src, writes=[stg[e % 2]])

            def pf_norm(e):
                for kk in range(2):
                    k = 2 * e + kk
                    dve_stt(hT[k].ap, stg[e % 2].ap[:, kk * TB:(kk + 1) * TB], gcol(G_MIX, k), rstd_n.ap,
                            ALU.mult, ALU.mult, [stg[e % 2], vecs, rstd_n], [hT[k]])

            for t in range(8):
                wd0 = wload(wdn_d, D, hf * FH, 11, t * NW)
                wd1 = wload(wdn_d, D, hf * FH + 11, 11, t * NW)
                for oc in range(2):
                    n = 2 * t + oc
                    g = n
                    if pre:
                        if 1 <= g <= 8:
                            for kk in range(2):
                                sqt = sqn[(2 * (g - 1) + kk) % 4]
                                act(sqt.ap, stg[(g - 1) % 2].ap[:, kk * TB:(kk + 1) * TB], AF.Square,
                                    [stg[(g - 1) % 2]], [sqt])
                        if g >= 10:
                            pf_norm(g - 10)
                        pf_dma(g)
                    cs = (oc * P, (oc + 1) * P)
                    bank = next_bank()
                    for i in range(FH):
                        w = wd0 if i < 11 else wd1
                        k = i if i < 11 else i - 11
                        mm(bank, (0, TB), w.lhsT(k, *cs), aT[i].ap, i == 0, i == FH - 1, [aT[i], w.tile(k)])
                    dve_tt(xT[n].ap, bank.ap, xT[n].ap, ALU.add, [bank, xT[n]], [xT[n]])
                    if fin_ov:
                        act(sqn[n % 4].ap, xT[n].ap, AF.Square, [xT[n]], [sqn[n % 4]])
                        if n >= 2:
                            mm(fin_bank, (0, TB), ones.ap, sqn[(n - 2) % 4].ap, n == 2, False,
                               [ones, sqn[(n - 2) % 4]])
                    if pre and 2 <= g <= 9:
                        e = g - 2
                        for kk in range(2):
                            sqt = sqn[(2 * e + kk) % 4]
                            mm(pf_bank, (0, TB), ones.ap, sqt.ap, e == 0 and kk == 0, e == 7 and kk == 1,
                               [ones, sqt])
                        if g == 9:
                            act(rt.ap, pf_bank.ap, AF.Sqrt, [pf_bank], [rt], scale=1.0 / D, bias=EPS)
                            sc.op("dve", lambda e_: e_.reciprocal(out=rstd_n.ap, in_=rt.ap), [rt], [rstd_n])
                            reserved.discard(pf_bank.name)
            if pre:
                pf_norm(6)
                pf_norm(7)

        def fin_dst(k):
            o = (ostg + sgA + sgB)[ocount[0] % 6]
            st = st_o[ocount[0] % 6]
            ocount[0] += 1
            dve_stt(o.ap, xT[k].ap, gcol(G_FIN, k), rstd.ap, ALU.mult, ALU.mult, [xT[k], vecs, rstd], [o])
            dma("sp", st, yT_d[k * P:(k + 1) * P, c0:c0 + TB], o.ap, reads=[o])
        if b + 1 == NB:
            for n in (KC - 2, KC - 1):
                mm(fin_bank, (0, TB), ones.ap, sqn[n % 4].ap, False, n == KC - 1, [ones, sqn[n % 4]])
            reserved.discard(fin_bank.name)
            act(rt.ap, fin_bank.ap, AF.Sqrt, [fin_bank], [rt], scale=1.0 / D, bias=EPS)
            sc.op("dve", lambda e: e.reciprocal(out=rstd.ap, in_=rt.ap), [rt], [rstd])
            for k in range(KC):
                fin_dst(k)
        else:
            rmsnorm(G_FIN, fin_dst)

    for st in st_o:
        if st.n > 0:
            sc.final_wait("sp", (st, st.n - 1))

    sc.finalize()
    with nc.allow_low_precision("bf16 matmul operands, fp32 accumulation"):
        with nc.Block() as block:
            @block.tensor
            def _(e):
                sc.replay("pe", e)

            @block.scalar
            def _(e):
                sc.replay("act", e)

            @block.vector
            def _(e):
                sc.replay("dve", e)

            @block.gpsimd
            def _(e):
                sc.replay("pool", e)

            @block.sync
            def _(e):
                sc.replay("sp", e)
    return nc


def _consts():
    p = np.arange(P)[:, None]
    f = np.arange(BTW)[None, :]
    dist = f - p
    idx = np.clip(dist, -128, 128) + 128
    vis = ((p // 64) <= (f // 64)) & ((f // 64) <= (p // 64) + 8)
    mask = np.where(vis, 0.0, MASKV).astype(np.float32)
    ident = np.eye(P, dtype=np.float32)
    invc = np.zeros((4, 16), np.float32)
    for g, w in enumerate((2, 4, 8, 16)):
        invc[g] = 1.0 / np.minimum(np.arange(16) + 1, w)
    return idx, mask, ident, invc


def _prep_shared(norm_mix, w_in, rel_bias, pool_w, pool_scale, w_branch_a, w_branch_b, w_out,
                 norm_ffn, w_gate_up, w_down, norm_final):
    idx, mask, ident, invc = _consts()
    f32 = np.float32
    bt = np.ascontiguousarray(np.transpose(np.asarray(rel_bias[0], f32)[:, idx], (1, 0, 2)))

    def cols(v, n):
        return np.asarray(v, f32).reshape(n, P).T

    vecs = np.concatenate([cols(norm_mix[0], KC), cols(norm_ffn[0], KC), cols(norm_final, KC),
                           cols(pool_scale[0], AC), np.broadcast_to(invc.reshape(1, 64), (P, 64))], axis=1)
    return {
        "w_in": np.ascontiguousarray(np.asarray(w_in[0], f32)),
        "pool_w": np.ascontiguousarray(np.asarray(pool_w[0], f32).reshape(AW, 256)),
        "w_a": np.ascontiguousarray(np.asarray(w_branch_a[0], f32)),
        "w_b": np.ascontiguousarray(np.asarray(w_branch_b[0], f32)),
        "w_o": np.ascontiguousarray(np.asarray(w_out[0], f32)),
        "w_gu": np.ascontiguousarray(np.asarray(w_gate_up[0], f32)),
        "w_dn": np.ascontiguousarray(np.asarray(w_down[0], f32)),
        "bt": bt, "mask": mask, "ident": ident,
        "vecs": np.ascontiguousarray(vecs.astype(f32)),
    }


_NC_CACHE = {}


def kernel(x, norm_mix, w_in, rel_bias, pool_w, pool_scale, w_branch_a, w_branch_b,
           w_out, norm_ffn, w_gate_up, w_down, norm_final):
    x = np.asarray(x, np.float32)
    B, S, _ = x.shape
    shared = _prep_shared(norm_mix, w_in, rel_bias, pool_w, pool_scale, w_branch_a, w_branch_b,
                          w_out, norm_ffn, w_gate_up, w_down, norm_final)
    if S not in _NC_CACHE:
        _NC_CACHE[S] = build_program(S)
    nc = _NC_CACHE[S]
    in_maps = []
    for c in range(B):
        m = dict(shared)
        m["xT"] = np.ascontiguousarray(x[c].T)
        in_maps.append(m)
    res = run_bass_kernel_spmd(nc, in_maps, core_ids=list(range(B)))
    out = np.stack([np.ascontiguousarray(r["yT"].T) for r in res.results], axis=0)
    return out.astype(np.float32)
```
